# Optimizing a Trainium2 kernel written in Bass

```python
import math
import jax, jax.numpy as jnp
from jax import lax
import numpy as np

D_MODEL = 1024
BATCH = 4
SEQ = 8192
DEPTH = 2

NORM_EPS = 1e-6
D_FF = 2816
D_MIX = D_MODEL
MLA_HEADS = 8
MLA_NOPE_DIM = 64
MLA_ROPE_DIM = 32
MLA_V_DIM = 64
MLA_QK_DIM = MLA_NOPE_DIM + MLA_ROPE_DIM
MLA_Q_RANK = 256
MLA_KV_RANK = 128
MLA_WIDTH = MLA_HEADS * MLA_V_DIM
ROPE_THETA = 10000.0
Q_BLOCK = 128
CONV_WIDTH = D_MIX // 4
CONV_KERNEL = 31
HY_WIDTH = D_MIX - MLA_WIDTH - CONV_WIDTH
HY_ORDER = 2
HY_SHORT_KERNEL = 3
HY_EMB_DIM = 33
HY_FILTER_DIM = 64
HY_FAST_DECAY_PCT = 0.3
HY_SLOW_DECAY_PCT = 1.5
HY_DECAY_TARGET = 1e-2
HY_FILTER_OUT = HY_ORDER * 2 * HY_WIDTH
IN_MLA_Q = MLA_Q_RANK
IN_MLA_KV = MLA_KV_RANK
IN_MLA_KPE = MLA_ROPE_DIM
IN_CONV = 2 * CONV_WIDTH
IN_HY = (HY_ORDER + 1) * HY_WIDTH
OFF_Q = 0
OFF_KV = OFF_Q + IN_MLA_Q
OFF_KPE = OFF_KV + IN_MLA_KV
OFF_CONV = OFF_KPE + IN_MLA_KPE
OFF_HY = OFF_CONV + IN_CONV
IN_COLS = OFF_HY + IN_HY

kernel_name = "hymba_mla_conformer_hyena_macaron"


def rmsnorm(x, g):
    xf = x.astype(jnp.float32)
    xf = xf * lax.rsqrt(jnp.mean(xf * xf, axis=-1, keepdims=True) + NORM_EPS)
    return (xf * g.astype(jnp.float32)).astype(x.dtype)


def layernorm(x, g, b):
    xf = x.astype(jnp.float32)
    mu = jnp.mean(xf, axis=-1, keepdims=True)
    var = jnp.mean(jnp.square(xf - mu), axis=-1, keepdims=True)
    y = (xf - mu) * lax.rsqrt(var + NORM_EPS)
    return (y * g.astype(jnp.float32) + b.astype(jnp.float32)).astype(x.dtype)


def swiglu(h, w_gate, w_up, w_down):
    return (jax.nn.silu(h @ w_gate) * (h @ w_up)) @ w_down


def depthwise_conv(u, w, b):
    k, c = w.shape
    pad = k // 2
    y = lax.conv_general_dilated(u, w[:, None, :].astype(u.dtype), window_strides=(1,),
                                 padding=[(pad, pad)], dimension_numbers=('NWC', 'WIO', 'NWC'),
                                 feature_group_count=c)
    return y + b.astype(u.dtype)


def rope_cos_sin(positions, dtype):
    inv_freq = 1.0 / (ROPE_THETA ** (jnp.arange(0, MLA_ROPE_DIM, 2, dtype=jnp.float32) / MLA_ROPE_DIM))
    ang = positions.astype(jnp.float32)[..., None] * inv_freq
    return jnp.cos(ang).astype(dtype), jnp.sin(ang).astype(dtype)


def apply_rope(x, cos, sin):
    x1, x2 = jnp.split(x, 2, axis=-1)
    return jnp.concatenate([x1 * cos - x2 * sin, x1 * sin + x2 * cos], axis=-1)


def mla_mixer(h_q, h_kv, k_pe, positions, q_norm, w_qb, kv_norm, w_kvb):
    b, l, _ = h_q.shape
    cos, sin = rope_cos_sin(positions, h_q.dtype)
    q = (rmsnorm(h_q, q_norm) @ w_qb).reshape(b, l, MLA_HEADS, MLA_QK_DIM)
    q_pe = apply_rope(q[..., MLA_NOPE_DIM:], cos[:, :, None], sin[:, :, None])
    q = jnp.concatenate([q[..., :MLA_NOPE_DIM], q_pe], axis=-1) * (MLA_QK_DIM ** -0.5)
    kv = (rmsnorm(h_kv, kv_norm) @ w_kvb).reshape(b, l, MLA_HEADS, MLA_NOPE_DIM + MLA_V_DIM)
    k_pe = apply_rope(k_pe, cos, sin)
    k = jnp.concatenate([kv[..., :MLA_NOPE_DIM],
                         jnp.broadcast_to(k_pe[:, :, None], (b, l, MLA_HEADS, MLA_ROPE_DIM))], axis=-1)
    v = kv[..., MLA_NOPE_DIM:]
    n_blk = l // Q_BLOCK
    q_blocks = q.reshape(b, n_blk, Q_BLOCK, MLA_HEADS, MLA_QK_DIM).transpose(1, 0, 2, 3, 4)

    def attend(qb):
        s = jnp.einsum('bqhd,bkhd->bhqk', qb, k).astype(jnp.float32)
        p = jax.nn.softmax(s, axis=-1).astype(v.dtype)
        return jnp.einsum('bhqk,bkhd->bqhd', p, v)

    o = lax.map(attend, q_blocks)
    return o.transpose(1, 0, 2, 3, 4).reshape(b, l, MLA_WIDTH)


def conformer_conv_mixer(h_conv, dw_w, dw_b, ln_g, ln_b):
    a, g = jnp.split(h_conv, 2, axis=-1)
    u = a * jax.nn.sigmoid(g)
    u = depthwise_conv(u, dw_w, dw_b)
    return jax.nn.silu(layernorm(u, ln_g, ln_b))


def hyena_filters(l, w1, b1, w2, b2, w3, b3, w4, freq):
    f32 = jnp.float32
    t = jnp.linspace(0.0, 1.0, l, dtype=f32)[:, None]
    bands = (HY_EMB_DIM - 1) // 2
    ang = 2.0 * math.pi * jnp.arange(l, dtype=f32)[:, None] / l
    fb = jnp.linspace(1e-4, bands - 1, bands, dtype=f32)[None, :]
    z = jnp.concatenate([t, jnp.cos(fb * ang), -jnp.sin(fb * ang)], axis=-1)
    fr = freq.astype(f32)
    hdn = jnp.sin(fr * (z @ w1.astype(f32) + b1.astype(f32)))
    hdn = jnp.sin(fr * (hdn @ w2.astype(f32) + b2.astype(f32)))
    hdn = jnp.sin(fr * (hdn @ w3.astype(f32) + b3.astype(f32)))
    h = (hdn @ w4.astype(f32)).reshape(l, HY_ORDER, 2, HY_WIDTH)
    max_decay = math.log(HY_DECAY_TARGET) / HY_FAST_DECAY_PCT
    min_decay = math.log(HY_DECAY_TARGET) / HY_SLOW_DECAY_PCT
    deltas = jnp.linspace(min_decay, max_decay, HY_WIDTH, dtype=f32)
    h = h * jnp.exp(-t * jnp.abs(deltas))[:, None, None, :]
    fwd, bwd = h[:, :, 0], h[:, :, 1]
    k = jnp.concatenate([fwd, jnp.zeros_like(fwd[:1]), jnp.flip(bwd[1:], axis=0)], axis=0)
    return k / jnp.sum(jnp.abs(k), axis=0, keepdims=True)


def fft_long_conv(u, k, d):
    l = u.shape[1]
    uf = u.astype(jnp.float32)
    uk = jnp.fft.rfft(uf, n=2 * l, axis=1) * jnp.fft.rfft(k, axis=0)[None]
    y = jnp.fft.irfft(uk, n=2 * l, axis=1)[:, :l]
    return (y + uf * d.astype(jnp.float32)).astype(u.dtype)


def hyena_mixer(h_hy, short_w, short_b, w1, b1, w2, b2, w3, b3, w4, freq, bias_d):
    u = depthwise_conv(h_hy, short_w, short_b)
    v, x1, x2 = jnp.split(u, HY_ORDER + 1, axis=-1)
    k = hyena_filters(h_hy.shape[1], w1, b1, w2, b2, w3, b3, w4, freq)
    z = x1 * fft_long_conv(v, k[:, 0], bias_d[0])
    return x2 * fft_long_conv(z, k[:, 1], bias_d[1])


def setup_inputs(seed: int = 0) -> dict:
    key = jax.random.key(seed)
    ks = iter(jax.random.split(key, 64))
    f32 = jnp.float32

    def nrm(shape, scale):
        return jax.random.normal(next(ks), shape, f32) * scale

    def gain(shape):
        return 1.0 + nrm(shape, 0.05)

    dd = DEPTH
    return {
        "x": nrm((BATCH, SEQ, D_MODEL), 1.0),
        "positions": jnp.broadcast_to(jnp.arange(SEQ, dtype=jnp.int32)[None], (BATCH, SEQ)),
        "ffn1_norm": gain((dd, D_MODEL)),
        "ffn1_w_gate": nrm((dd, D_MODEL, D_FF), D_MODEL ** -0.5),
        "ffn1_w_up": nrm((dd, D_MODEL, D_FF), D_MODEL ** -0.5),
        "ffn1_w_down": nrm((dd, D_FF, D_MODEL), D_FF ** -0.5),
        "mix_norm": gain((dd, D_MODEL)),
        "w_in": nrm((dd, D_MODEL, IN_COLS), D_MODEL ** -0.5),
        "mla_q_norm": gain((dd, MLA_Q_RANK)),
        "mla_w_qb": nrm((dd, MLA_Q_RANK, MLA_HEADS * MLA_QK_DIM), MLA_Q_RANK ** -0.5),
        "mla_kv_norm": gain((dd, MLA_KV_RANK)),
        "mla_w_kvb": nrm((dd, MLA_KV_RANK, MLA_HEADS * (MLA_NOPE_DIM + MLA_V_DIM)), MLA_KV_RANK ** -0.5),
        "conv_dw_w": nrm((dd, CONV_KERNEL, CONV_WIDTH), CONV_KERNEL ** -0.5),
        "conv_dw_b": nrm((dd, CONV_WIDTH), 0.02),
        "conv_ln_g": gain((dd, CONV_WIDTH)),
        "conv_ln_b": nrm((dd, CONV_WIDTH), 0.02),
        "hy_short_w": nrm((dd, HY_SHORT_KERNEL, IN_HY), HY_SHORT_KERNEL ** -0.5),
        "hy_short_b": nrm((dd, IN_HY), 0.02),
        "hy_filt_w1": nrm((dd, HY_EMB_DIM, HY_FILTER_DIM), HY_EMB_DIM ** -0.5),
        "hy_filt_b1": nrm((dd, HY_FILTER_DIM), 0.1),
        "hy_filt_w2": nrm((dd, HY_FILTER_DIM, HY_FILTER_DIM), HY_FILTER_DIM ** -0.5),
        "hy_filt_b2": nrm((dd, HY_FILTER_DIM), 0.1),
        "hy_filt_w3": nrm((dd, HY_FILTER_DIM, HY_FILTER_DIM), HY_FILTER_DIM ** -0.5),
        "hy_filt_b3": nrm((dd, HY_FILTER_DIM), 0.1),
        "hy_filt_w4": nrm((dd, HY_FILTER_DIM, HY_FILTER_OUT), HY_FILTER_DIM ** -0.5),
        "hy_filt_freq": gain((dd, HY_FILTER_DIM)),
        "hy_bias_d": nrm((dd, HY_ORDER, HY_WIDTH), 0.5),
        "out_norm": gain((dd, D_MIX)),
        "w_out": nrm((dd, D_MIX, D_MODEL), D_MIX ** -0.5),
        "ffn2_norm": gain((dd, D_MODEL)),
        "ffn2_w_gate": nrm((dd, D_MODEL, D_FF), D_MODEL ** -0.5),
        "ffn2_w_up": nrm((dd, D_MODEL, D_FF), D_MODEL ** -0.5),
        "ffn2_w_down": nrm((dd, D_FF, D_MODEL), D_FF ** -0.5),
        "final_norm": gain((D_MODEL,)),
    }


def reference(x, positions, ffn1_norm, ffn1_w_gate, ffn1_w_up, ffn1_w_down, mix_norm, w_in,
              mla_q_norm, mla_w_qb, mla_kv_norm, mla_w_kvb, conv_dw_w, conv_dw_b, conv_ln_g, conv_ln_b,
              hy_short_w, hy_short_b, hy_filt_w1, hy_filt_b1, hy_filt_w2, hy_filt_b2, hy_filt_w3, hy_filt_b3,
              hy_filt_w4, hy_filt_freq, hy_bias_d, out_norm, w_out, ffn2_norm, ffn2_w_gate, ffn2_w_up,
              ffn2_w_down, final_norm):
    e1 = MLA_WIDTH
    e2 = MLA_WIDTH + CONV_WIDTH
    for i in range(DEPTH):
        x = x + 0.5 * swiglu(rmsnorm(x, ffn1_norm[i]), ffn1_w_gate[i], ffn1_w_up[i], ffn1_w_down[i])
        h = rmsnorm(x, mix_norm[i]) @ w_in[i]
        y_mla = mla_mixer(h[..., OFF_Q:OFF_KV], h[..., OFF_KV:OFF_KPE], h[..., OFF_KPE:OFF_CONV], positions,
                          mla_q_norm[i], mla_w_qb[i], mla_kv_norm[i], mla_w_kvb[i])
        y_conv = conformer_conv_mixer(h[..., OFF_CONV:OFF_HY], conv_dw_w[i], conv_dw_b[i],
                                      conv_ln_g[i], conv_ln_b[i])
        y_hy = hyena_mixer(h[..., OFF_HY:], hy_short_w[i], hy_short_b[i], hy_filt_w1[i], hy_filt_b1[i],
                           hy_filt_w2[i], hy_filt_b2[i], hy_filt_w3[i], hy_filt_b3[i], hy_filt_w4[i],
                           hy_filt_freq[i], hy_bias_d[i])
        g = out_norm[i]
        y = jnp.concatenate([rmsnorm(y_mla, g[:e1]), rmsnorm(y_conv, g[e1:e2]), rmsnorm(y_hy, g[e2:])], axis=-1)
        x = x + y @ w_out[i]
        x = x + 0.5 * swiglu(rmsnorm(x, ffn2_norm[i]), ffn2_w_gate[i], ffn2_w_up[i], ffn2_w_down[i])
    return rmsnorm(x, final_norm)
```

```python
import contextlib
import numpy as np
import concourse.bass as bass
import concourse.mybir as mybir

F32 = mybir.dt.float32
BF16 = mybir.dt.bfloat16
I32 = mybir.dt.int32
AF = mybir.ActivationFunctionType
ALU = mybir.AluOpType
AX = mybir.AxisListType


class Buf:
    __slots__ = ("name", "w", "r")

    def __init__(self, name):
        self.name = name
        self.w = None
        self.r = []


class View:
    __slots__ = ("ap", "bufs")

    def __init__(self, ap, bufs):
        self.ap = ap
        self.bufs = bufs

    def __getitem__(self, idx):
        return View(self.ap[idx], self.bufs)

    def re(self, pattern, **kw):
        return View(self.ap.rearrange(pattern, **kw), self.bufs)

    def bc(self, shape):
        return View(self.ap.broadcast_to(shape), self.bufs)


class Eng:
    def __init__(self, name, h, sem):
        self.name = name
        self.h = h
        self.sem = sem
        self.cnt = 0
        self.seen = {}
        self.pending = False


class Prog:
    def __init__(self, nc, n_dma_sems=24):
        self.nc = nc
        self.es = contextlib.ExitStack()
        self.E = {}
        for nm, h in (("pe", nc.tensor), ("act", nc.scalar), ("dve", nc.vector),
                      ("pool", nc.gpsimd), ("sp", nc.sync)):
            sem = self.es.enter_context(nc.semaphore("s_" + nm))
            self.E[nm] = Eng(nm, h, sem)
        self.dsems = {}
        self.dnext = {}
        for q, n in (("sp", 14), ("pool", 14), ("act", 4)):
            self.dsems[q] = []
            self.dnext[q] = 0
            for i in range(n):
                sem = self.es.enter_context(nc.semaphore("d%s%d" % (q, i)))
                self.dsems[q].append([sem, 0])
        self.spare = [self.es.enter_context(nc.semaphore("sp%d" % i)) for i in range(40)]
        self.uid = 0
        self.out_events = []

    def _nm(self, name):
        self.uid += 1
        return "%s_%d" % (name, self.uid)

    def sb(self, name, shape, dtype, es=None):
        name = self._nm(name)
        t = (es or self.es).enter_context(self.nc.sbuf_tensor(name, list(shape), dtype))
        return View(t.ap() if hasattr(t, "ap") else t[:], [Buf(name)])

    def sbn(self, name, shape, dtype, n, es=None):
        shp = [shape[0], n] + list(shape[1:])
        name = self._nm(name)
        t = (es or self.es).enter_context(self.nc.sbuf_tensor(name, shp, dtype))
        ap = t.ap() if hasattr(t, "ap") else t[:]
        return [View(ap[:, i], [Buf("%s%d" % (name, i))]) for i in range(n)]

    def ps(self, name, shape, dtype, es=None):
        name = self._nm(name)
        t = (es or self.es).enter_context(self.nc.psum_tensor(name, list(shape), dtype))
        ap = t.ap() if hasattr(t, "ap") else t[:]
        return View(ap, [Buf(name)])

    def _wait(self, e, ev):
        if ev is None:
            return
        key, sem, val = ev
        if e.seen.get(key, 0) >= val:
            return
        e.h.wait_ge(sem, val)
        e.seen[key] = val

    def _deps(self, e, reads, writes, pe_acc=False):
        for v in reads:
            for b in v.bufs:
                self._wait(e, b.w)
        for v in writes:
            for b in v.bufs:
                if not (pe_acc and b.w is not None and b.w[0].startswith("pe:")):
                    self._wait(e, b.w)
                for ev in b.r:
                    self._wait(e, ev)

    def _record(self, ev, reads, writes):
        for v in reads:
            for b in v.bufs:
                b.r.append(ev)
                if len(b.r) > 12:
                    best = {}
                    for x in b.r:
                        if x[0] not in best or best[x[0]][2] < x[2]:
                            best[x[0]] = x
                    b.r = list(best.values())
        for v in writes:
            for b in v.bufs:
                b.w = ev
                b.r = []

    def op(self, eng, fn, *, reads=(), writes=(), signal=True, pe_acc=False, **kw):
        e = self.E[eng]
        rd = list(reads)
        wr = list(writes)
        args = {}
        for k, v in kw.items():
            if isinstance(v, View):
                if k in ("out", "accum_out"):
                    wr.append(v)
                else:
                    rd.append(v)
                args[k] = v.ap
            else:
                args[k] = v
        self._deps(e, rd, wr, pe_acc=pe_acc)
        ins = getattr(e.h, fn)(**args)
        if signal:
            ins.then_inc(e.sem, 1)
            e.cnt += 1
            ev = ("%s:%d" % (eng, e.sem.num), e.sem, e.cnt)
        else:
            ev = ("%s:%d" % (eng, e.sem.num), e.sem, e.cnt + 1)
        self._record(ev, rd, wr)
        return ev

    def mm(self, out, lhsT, rhs, start, stop, **kw):
        return self.op("pe", "matmul", out=out, lhsT=lhsT, rhs=rhs, start=start, stop=stop,
                       signal=bool(stop), pe_acc=True, **kw)

    def transpose(self, out, in_, identity, signal=True):
        return self.op("pe", "transpose", out=out, in_=in_, identity=identity, signal=signal, pe_acc=True)

    def dma(self, q, out, in_, dram_out=False, **kw):
        e = self.E[q]
        slot = self.dsems[q][self.dnext[q]]
        self.dnext[q] = (self.dnext[q] + 1) % len(self.dsems[q])
        sem, val = slot
        key = "d%d" % sem.num
        if val > 0:
            self._wait(e, (key, sem, val))
        rd, wr = [], []
        o = out.ap if isinstance(out, View) else out
        i = in_.ap if isinstance(in_, View) else in_
        if isinstance(out, View):
            wr.append(out)
        if isinstance(in_, View):
            rd.append(in_)
        self._deps(e, rd, wr)
        e.h.dma_start(out=o, in_=i, **kw).then_inc(sem, 16)
        slot[1] = val + 16
        ev = (key, sem, val + 16)
        self._record(ev, rd, wr)
        if dram_out:
            self.out_events.append(ev)
        return ev

    def wait_event(self, eng, ev):
        self._wait(self.E[eng], ev)

    def finish(self, eng="sp"):
        e = self.E[eng]
        for ev in self.out_events:
            self._wait(e, ev)
        self.out_events = []

    def barrier(self):
        evs = []
        for nm, e in self.E.items():
            if e.cnt > 0:
                evs.append(("%s:%d" % (nm, e.sem.num), e.sem, e.cnt))
        for q in self.dsems:
            for sem, val in self.dsems[q]:
                if val > 0:
                    evs.append(("d%d" % sem.num, sem, val))
        for nm, e in self.E.items():
            for ev in evs:
                self._wait(e, ev)
        for nm, e in self.E.items():
            if e.cnt > 12000 and self.spare:
                e.sem = self.spare.pop()
                e.cnt = 0

    def collective(self, kind, in_ap, out_ap, groups):
        self.barrier()
        e = self.E["pool"]
        slot = self.dsems["pool"][self.dnext["pool"]]
        self.dnext["pool"] = (self.dnext["pool"] + 1) % len(self.dsems["pool"])
        sem, val = slot
        key = "d%d" % sem.num
        if val > 0:
            self._wait(e, (key, sem, val))
        ins = e.h.collective_compute(kind, op=ALU.bypass, replica_groups=groups, ins=[in_ap], outs=[out_ap])
        ins.then_inc(sem, 16)
        slot[1] = val + 16
        self.barrier()

    def close(self):
        self.es.close()
D = 1024
DFF = 2816
NFC = 22
EPS = 1e-6


def make_ident(P, es, dtype=BF16, name="ident"):
    ident = P.sb(name, [128, 128], dtype, es=es)
    P.op("pool", "memset", ap=ident.ap, constant=0.0, writes=[ident])
    P.op("pool", "affine_select", out=ident, in_=ident, pattern=[[-1, 128]],
         compare_op=ALU.not_equal, fill=1.0, base=0, channel_multiplier=1)
    return ident


def load_mask(P, es, mk_ap):
    mk = P.sb("mk", [128, 2], F32, es=es)
    P.dma("sp", mk, mk_ap.partition_broadcast(128))
    return mk


def blend(P, dst, alt, mk):
    np_ = dst.ap.shape[0]
    P.op("dve", "tensor_scalar", out=dst, in0=dst, scalar1=mk[0:np_, 0:1], scalar2=None, op0=ALU.mult)
    P.op("dve", "scalar_tensor_tensor", out=dst, in0=alt, scalar=mk[0:np_, 1:2], in1=dst, op0=ALU.mult, op1=ALU.add)


def rms_scale(P, ss, rs):
    P.op("dve", "tensor_scalar", out=rs, in0=ss, scalar1=EPS, scalar2=None, op0=ALU.add)
    P.op("act", "activation", out=rs, in_=rs, func=AF.Sqrt)
    P.op("dve", "reciprocal", out=rs, in_=rs)


def norm_transpose(P, xt, gB, xn, junk, ss, rs, pt, xnT_dst, ident, nd=8, groups=None):
    n = nd * 128
    groups = groups or [(0, n)]
    ng = len(groups)
    for gi, (a, b) in enumerate(groups):
        P.op("act", "activation", out=xn[:, a:b], in_=xt[:, a:b], func=AF.Square, scale=float((b - a) ** -0.5),
             accum_out=ss[:, gi:gi + 1])
    rms_scale(P, ss[:, :ng], rs[:, :ng])
    for gi, (a, b) in enumerate(groups):
        P.op("dve", "scalar_tensor_tensor", out=xn[:, a:b], in0=xt[:, a:b], scalar=rs[:, gi:gi + 1], in1=gB[:, a:b],
             op0=ALU.mult, op1=ALU.mult)
    for dc in range(nd):
        P.transpose(pt[:, dc, :], xn[:, dc * 128:(dc + 1) * 128], ident, signal=(dc == nd - 1))
    P.op("act", "copy", out=xnT_dst, in_=pt[:, :nd, :])


def stage_ffn(P, x_in, x_out, g_norm, wg, wu, wd, ntok, g_final=None):
    es = contextlib.ExitStack()
    NB = ntok // 512
    Wg = P.sbn("Wg", [128, DFF], BF16, 8, es=es)
    Wu = P.sbn("Wu", [128, DFF], BF16, 8, es=es)
    Wd = P.sbn("Wd", [128, D], BF16, NFC, es=es)
    gB = P.sb("gB", [128, D], F32, es=es)
    P.dma("sp", gB, g_norm.partition_broadcast(128))
    if g_final is not None:
        gF = P.sb("gF", [128, D], F32, es=es)
        P.dma("sp", gF, g_final.partition_broadcast(128))
    for dc in range(8):
        P.dma("pool", Wg[dc], wg[dc * 128:(dc + 1) * 128, :])
        P.dma("pool", Wu[dc], wu[dc * 128:(dc + 1) * 128, :])
    for fc in range(NFC):
        P.dma("pool", Wd[fc], wd[fc * 128:(fc + 1) * 128, :])
    ident = make_ident(P, es)
    NX = 3
    xts = P.sbn("xts", [128, D], F32, NX, es=es)
    xn = P.sbn("xn", [128, D], BF16, 2, es=es)
    junk = None
    ss = P.sbn("ss", [128, 4], F32, 2, es=es)
    rs = P.sbn("rs", [128, 4], F32, 2, es=es)
    xnT = P.sbn("xnT", [128, 8, 512], BF16, 2, es=es)
    gT = P.sbn("gT", [128, NFC, 512], BF16, 1, es=es)
    sg = P.sbn("sg", [128, 512], F32, 2, es=es)
    xo = P.sbn("xo", [128, D], F32, 2, es=es)
    pt = P.ps("pt", [128, 8, 128], BF16, es=es)
    pg = [P.ps("pg%d" % i, [128, 512], F32, es=es) for i in range(2)]
    pu = [P.ps("pu%d" % i, [128, 512], F32, es=es) for i in range(2)]
    po = [P.ps("po%d" % i, [128, 512], F32, es=es) for i in range(2)]
    ntile = ntok // 128

    def load_tile(t):
        if t < ntile:
            P.dma("sp", xts[t % NX], x_in[t * 128:(t + 1) * 128, :])

    for t in range(NX - 1):
        load_tile(t)
    cnt = 0
    for b in range(NB):
        s = b % 2
        for tt in range(4):
            t = b * 4 + tt
            load_tile(t + NX - 1)
            k = t % 2
            norm_transpose(P, xts[t % NX], gB, xn[k], junk, ss[k], rs[k], pt,
                           xnT[s][:, :, tt * 128:(tt + 1) * 128], ident)
        for fc in range(NFC):
            k = fc % 2
            for dc in range(8):
                P.mm(pg[k], Wg[dc][:, fc * 128:(fc + 1) * 128], xnT[s][:, dc, :], dc == 0, dc == 7)
            for dc in range(8):
                P.mm(pu[k], Wu[dc][:, fc * 128:(fc + 1) * 128], xnT[s][:, dc, :], dc == 0, dc == 7)
            P.op("act", "activation", out=sg[k], in_=pg[k], func=AF.Silu)
            P.op("dve", "tensor_tensor", out=gT[0][:, fc, :], in0=sg[k], in1=pu[k], op=ALU.mult)
        for tt in range(4):
            t = b * 4 + tt
            k2 = t % 2
            P.dma("sp", xo[k2], x_in[t * 128:(t + 1) * 128, :])
            for hf in range(2):
                k = cnt % 2
                cnt += 1
                for fc in range(NFC):
                    P.mm(po[k], gT[0][:, fc, tt * 128:(tt + 1) * 128], Wd[fc][:, hf * 512:(hf + 1) * 512],
                         fc == 0, fc == NFC - 1)
                P.op("dve", "scalar_tensor_tensor", out=xo[k2][:, hf * 512:(hf + 1) * 512], in0=po[k], scalar=0.5,
                     in1=xo[k2][:, hf * 512:(hf + 1) * 512], op0=ALU.mult, op1=ALU.add)
            if g_final is not None:
                P.op("act", "activation", out=xn[k2], in_=xo[k2], func=AF.Square, scale=1.0 / 32, accum_out=ss[k2][:, 0:1])
                rms_scale(P, ss[k2][:, 0:1], rs[k2][:, 0:1])
                P.op("dve", "scalar_tensor_tensor", out=xo[k2], in0=xo[k2], scalar=rs[k2][:, 0:1], in1=gF,
                     op0=ALU.mult, op1=ALU.mult)
            P.dma("pool", x_out[t * 128:(t + 1) * 128, :], xo[k2], dram_out=True)
    P.barrier()
    es.close()


NTM = 416
NFM = 1280


def stage_win(P, x_in, g_norm, w_in, h_tm, h_fm, ntok):
    es = contextlib.ExitStack()
    NB = ntok // 512
    W = P.sbn("Win", [128, 1696], BF16, 8, es=es)
    gB = P.sb("gB2", [128, D], F32, es=es)
    P.dma("sp", gB, g_norm.partition_broadcast(128))
    for dc in range(8):
        P.dma("pool", W[dc], w_in[dc * 128:(dc + 1) * 128, :])
    ident = make_ident(P, es)
    xts = P.sbn("xts", [128, D], F32, 8, es=es)
    xn = P.sbn("xn", [128, D], BF16, 2, es=es)
    junk = None
    ss = P.sbn("ss", [128, 4], F32, 2, es=es)
    rs = P.sbn("rs", [128, 4], F32, 2, es=es)
    xnT = P.sbn("xnT", [128, 8, 512], BF16, 2, es=es)
    ho = P.sbn("ho", [128, 512], F32, 4, es=es)
    pt = P.ps("pt", [128, 8, 128], BF16, es=es)
    pp = [P.ps("pp%d" % i, [128, 512], F32, es=es) for i in range(4)]

    def load_block(b):
        for tt in range(4):
            P.dma("sp", xts[(b % 2) * 4 + tt], x_in[b * 512 + tt * 128: b * 512 + (tt + 1) * 128, :])

    load_block(0)
    cnt = 0
    for b in range(NB):
        s = b % 2
        if b + 1 < NB:
            load_block(b + 1)
        for tt in range(4):
            k = (b * 4 + tt) % 2
            norm_transpose(P, xts[s * 4 + tt], gB, xn[k], junk, ss[k], rs[k], pt,
                           xnT[s][:, :, tt * 128:(tt + 1) * 128], ident)
        for tt in range(4):
            k = cnt % 4
            cnt += 1
            for dc in range(8):
                P.mm(pp[k][:, :NTM], xnT[s][:, dc, tt * 128:(tt + 1) * 128], W[dc][:, 0:NTM], dc == 0, dc == 7)
            P.op("act", "copy", out=ho[k][:, :NTM], in_=pp[k][:, :NTM])
            P.dma("pool", h_tm[b * 512 + tt * 128: b * 512 + (tt + 1) * 128, :], ho[k][:, :NTM], dram_out=True)
        for cc in range(NFM // 128):
            k = cnt % 4
            cnt += 1
            for dc in range(8):
                P.mm(pp[k], W[dc][:, NTM + cc * 128: NTM + (cc + 1) * 128], xnT[s][:, dc, :], dc == 0, dc == 7)
            if cc % 2 == 0:
                P.op("act", "copy", out=ho[k], in_=pp[k])
            else:
                P.op("dve", "tensor_copy", out=ho[k], in_=pp[k])
            P.dma("pool", h_fm[cc * 128:(cc + 1) * 128, b * 512:(b + 1) * 512], ho[k], dram_out=True)
    P.barrier()
    es.close()


def stage_wout(P, x_in, x_out, y_parts, g_out, w_out, ntok, y_cm=None, sel=None):
    es = contextlib.ExitStack()
    W = P.sbn("Wout", [128, D], BF16, 8, es=es)
    gB = P.sb("gB3", [128, D], F32, es=es)
    P.dma("sp", gB, g_out.partition_broadcast(128))
    for dc in range(8):
        P.dma("pool", W[dc], w_out[dc * 128:(dc + 1) * 128, :])
    ident = make_ident(P, es)
    yt = P.sbn("yt", [128, D], F32, 3, es=es)
    xn = P.sbn("xn", [128, D], BF16, 2, es=es)
    ss = P.sbn("ss", [128, 4], F32, 2, es=es)
    rs = P.sbn("rs", [128, 4], F32, 2, es=es)
    ynT = P.sbn("ynT", [128, 8, 128], BF16, 2, es=es)
    xo = P.sbn("xo", [128, D], F32, 2, es=es)
    pt = P.ps("pt", [128, 8, 128], BF16, es=es)
    po = [P.ps("po%d" % i, [128, 512], F32, es=es) for i in range(4)]
    ntile = ntok // 128
    groups = [(o, o + w) for (_, o, w) in y_parts]
    if y_cm is not None:
        identf = make_ident(P, es, F32, "identf")
        ncm = y_cm[2] // 128
        ycm = P.sbn("ycm", [128, ncm, 128], F32, 3, es=es)
        pcm = P.ps("pcm", [128, ncm, 128], F32, es=es)
        groups.append((y_cm[1], y_cm[1] + y_cm[2]))

    if sel is not None:
        mk = load_mask(P, es, sel[0])
        off = sel[1]
        xo2 = P.sbn("xo2", [128, D], F32, 2, es=es)
        if y_cm is not None:
            ycm2 = P.sbn("ycm2", [128, ncm, 128], F32, 2, es=es)

    def load_tile(t):
        if t < ntile:
            for (ap, o, w) in y_parts:
                P.dma("sp", yt[t % 3][:, o:o + w], ap[t * 128:(t + 1) * 128, :])
            if y_cm is not None:
                P.dma("sp", ycm[t % 3], y_cm[0][:, t * 128:(t + 1) * 128].rearrange("(c p) t -> p c t", p=128))
                if sel is not None:
                    P.dma("sp", ycm2[t % 2], y_cm[0][:, off + t * 128:off + (t + 1) * 128].rearrange("(c p) t -> p c t", p=128))
                    blend(P, ycm[t % 3], ycm2[t % 2], mk)

    load_tile(0)
    load_tile(1)
    cnt = 0
    for t in range(ntile):
        load_tile(t + 2)
        k = t % 2
        P.dma("sp", xo[k], x_in[t * 128:(t + 1) * 128, :])
        if sel is not None:
            P.dma("sp", xo2[k], x_in[off + t * 128:off + (t + 1) * 128, :])
            blend(P, xo[k], xo2[k], mk)
        if y_cm is not None:
            for c in range(ncm):
                P.transpose(pcm[:, c, :], ycm[t % 3][:, c, :], identf, signal=(c == ncm - 1))
            P.op("act", "copy", out=yt[t % 3][:, y_cm[1]:y_cm[1] + y_cm[2]], in_=pcm.re("p c k -> p (c k)"))
        norm_transpose(P, yt[t % 3], gB, xn[k], None, ss[k], rs[k], pt, ynT[k], ident, groups=groups)
        for hf in range(2):
            kk = cnt % 4
            cnt += 1
            for dc in range(8):
                P.mm(po[kk], ynT[k][:, dc, :], W[dc][:, hf * 512:(hf + 1) * 512], dc == 0, dc == 7)
            P.op("dve", "tensor_tensor", out=xo[k][:, hf * 512:(hf + 1) * 512], in0=po[kk],
                 in1=xo[k][:, hf * 512:(hf + 1) * 512], op=ALU.add)
        P.dma("pool", x_out[t * 128:(t + 1) * 128, :], xo[k], dram_out=True)
    P.barrier()
    es.close()
import math
NH = 8
STOP = 0
S1PART = 0
QK = 96
PI = 3.14159265358979


def sin_rr(P, out, x, ti, tf, off=0.0):
    C1 = 6.28125
    C2 = 2 * math.pi - C1
    P.op("dve", "tensor_scalar", out=tf, in0=x, scalar1=1.0 / (2 * math.pi), scalar2=off / (2 * math.pi), op0=ALU.mult, op1=ALU.add)
    P.op("dve", "tensor_copy", out=ti, in_=tf)
    P.op("dve", "tensor_copy", out=tf, in_=ti)
    P.op("dve", "scalar_tensor_tensor", out=out, in0=tf, scalar=-C1, in1=x, op0=ALU.mult, op1=ALU.add)
    P.op("dve", "scalar_tensor_tensor", out=out, in0=tf, scalar=-C2, in1=out, op0=ALU.mult, op1=ALU.add)
    P.op("dve", "tensor_scalar", out=out, in0=out, scalar1=off, scalar2=-3.14159, op0=ALU.add, op1=ALU.max)
    P.op("dve", "tensor_scalar", out=out, in0=out, scalar1=3.14159, scalar2=None, op0=ALU.min)
    P.op("act", "activation", out=out, in_=out, func=AF.Sin)


def rope_tables(P, pos_i, posf, invfB, ang, cs, sn, ti, tf):
    P.op("dve", "tensor_copy", out=posf, in_=pos_i)
    P.op("dve", "tensor_scalar", out=ang, in0=invfB, scalar1=posf, scalar2=None, op0=ALU.mult)
    sin_rr(P, sn, ang, ti, tf, 0.0)
    sin_rr(P, cs, ang, ti, tf, math.pi / 2)


def rope_tables_all(P, es, pos_ap, NT_, invfB, cs_all, sn_all, tag):
    pi_ = P.sb("rpi" + tag, [128, NT_], I32, es=es)
    P.dma("sp", pi_, pos_ap.rearrange("(t p) o -> p (t o)", p=128), allow_slow_non_contiguous=True)
    pf_ = P.sb("rpf" + tag, [128, NT_], F32, es=es)
    P.op("dve", "tensor_copy", out=pf_, in_=pi_)
    CH = 16
    ang = P.sb("rang" + tag, [128, CH, 16], F32, es=es)
    ti = P.sb("rti" + tag, [128, CH * 16], I32, es=es)
    tf = P.sb("rtf" + tag, [128, CH * 16], F32, es=es)
    for t0 in range(0, NT_, CH):
        n = min(CH, NT_ - t0)
        P.op("dve", "tensor_tensor", out=ang[:, 0:n, :],
             in0=pf_[:, t0:t0 + n].re("p (t o) -> p t o", o=1).bc([128, n, 16]),
             in1=invfB.re("p (o d) -> p o d", o=1).bc([128, n, 16]), op=ALU.mult)
        af = ang[:, 0:n, :].re("p t d -> p (t d)")
        sin_rr(P, sn_all[:, t0:t0 + n, :].re("p t d -> p (t d)"), af, ti[:, 0:n * 16], tf[:, 0:n * 16], 0.0)
        sin_rr(P, cs_all[:, t0:t0 + n, :].re("p t d -> p (t d)"), af, ti[:, 0:n * 16], tf[:, 0:n * 16], math.pi / 2)


def stage_mla(P, h_q, h_kv, pos_k, pos_q, invf, g_q, w_qb, g_kv, w_kvb, y_out, NQ, NK, sel=None, HG=8):
    es = contextlib.ExitStack()
    NKT = NK // 128
    NQT = NQ // 128
    NQB = NQ // 512
    SC = float(QK ** -0.5)
    identb = make_ident(P, es, BF16, "identb")
    identf = make_ident(P, es, F32, "identf")
    Wq = P.sbn("Wq", [128, 768], BF16, 2, es=es)
    for c in range(2):
        P.dma("pool", Wq[c], w_qb[c * 128:(c + 1) * 128, :])
    Wkv = P.sb("Wkv", [128, 1024], BF16, es=es)
    P.dma("pool", Wkv, w_kvb)
    gq = P.sb("gq", [128, 256], F32, es=es)
    P.dma("sp", gq, g_q.partition_broadcast(128))
    gkv = P.sb("gkv", [128, 128], F32, es=es)
    P.dma("sp", gkv, g_kv.partition_broadcast(128))
    invfB = P.sb("invfB", [128, 16], F32, es=es)
    P.dma("sp", invfB, invf.partition_broadcast(128))
    kvnT = P.sb("kvnT", [128, NK], BF16, es=es)
    kT = P.sbn("kT", [128, NK], BF16, 2, es=es)
    vA = P.sbn("vA", [128, NKT, 65], BF16, 2, es=es)
    qT = P.sb("qT", [128, HG, NQ], BF16, es=es)
    for s in range(2):
        P.op("pool", "memset", ap=kT[s][96:97, :].ap, constant=1.0, writes=[kT[s]])
        P.op("pool", "memset", ap=vA[s][:, :, 64:65].ap, constant=1.0, writes=[vA[s]])
    ptb = [P.ps("ptb%d" % i, [128, 8, 128], BF16, es=es) for i in range(2)]
    pss2 = [P.ps("pss%d" % i, [128, 2, 512], F32, es=es) for i in range(2)]
    pss = [pss2[0][:, 0, :], pss2[0][:, 1, :], pss2[1][:, 0, :], pss2[1][:, 1, :]]
    pso = [P.ps("pso0", [128, 512], F32, es=es)]
    pmisc = P.ps("pmisc", [128, 512], F32, es=es)

    if STOP == -1:
        P.barrier(); es.close(); return
    es1 = es
    kpeT = P.sb("kpeT", [32, NK], BF16, es=es1)
    hk = P.sbn("hk", [128, 160], F32, 3, es=es1)
    pk = P.sbn("pk", [128, 1], I32, 3, es=es1)
    kb = P.sbn("kb", [128, 256], BF16, 2, es=es1)
    for k in range(2):
        P.op("pool", "memset", ap=kb[k].ap, constant=0.0, writes=[kb[k]])
    ss = P.sbn("ss", [128, 4], F32, 2, es=es1)
    rs = P.sbn("rs", [128, 4], F32, 2, es=es1)
    posf = P.sbn("posf", [128, 1], F32, 2, es=es1)
    ang = P.sbn("ang", [128, 16], F32, 2, es=es1)
    cs = P.sbn("cs", [128, 16], F32, 2, es=es1)
    sn = P.sbn("sn", [128, 16], F32, 2, es=es1)
    ti = P.sbn("ti", [128, 16], I32, 2, es=es1)
    tf = P.sbn("tf", [128, 16], F32, 2, es=es1)
    t1 = P.sbn("t1", [128, 16], F32, 2, es=es1)
    t2 = P.sbn("t2", [128, 16], F32, 2, es=es1)

    NTM = max(NKT, NQT)
    cs_all = P.sb("cs_all", [128, NTM, 16], F32, es=es)
    sn_all = P.sb("sn_all", [128, NTM, 16], F32, es=es)
    rope_tables_all(P, es, pos_k, NKT, invfB, cs_all, sn_all, "k")

    def loadk(t):
        if t < NKT:
            P.dma("sp", hk[t % 3], h_kv[t * 128:(t + 1) * 128, :])

    loadk(0)
    loadk(1)
    for t in range(NKT):
        loadk(t + 2)
        k = t % 2
        x = hk[t % 3]
        csk = cs_all[:, t, :]
        snk = sn_all[:, t, :]
        P.op("act", "activation", out=kb[k][:, 0:128], in_=x[:, 0:128], func=AF.Square, scale=float(128 ** -0.5),
             accum_out=ss[k][:, 0:1])
        rms_scale(P, ss[k][:, 0:1], rs[k][:, 0:1])
        P.op("dve", "scalar_tensor_tensor", out=kb[k][:, 0:128], in0=x[:, 0:128], scalar=rs[k][:, 0:1], in1=gkv,
             op0=ALU.mult, op1=ALU.mult)
        if S1PART == 2:
            continue
        x1 = x[:, 128:144]
        x2 = x[:, 144:160]
        P.op("dve", "tensor_tensor", out=t1[k], in0=x1, in1=csk, op=ALU.mult)
        P.op("dve", "tensor_tensor", out=t2[k], in0=x2, in1=snk, op=ALU.mult)
        P.op("dve", "tensor_tensor", out=kb[k][:, 128:144], in0=t1[k], in1=t2[k], op=ALU.subtract)
        P.op("dve", "tensor_tensor", out=t1[k], in0=x1, in1=snk, op=ALU.mult)
        P.op("dve", "tensor_tensor", out=t2[k], in0=x2, in1=csk, op=ALU.mult)
        P.op("dve", "tensor_tensor", out=kb[k][:, 144:160], in0=t1[k], in1=t2[k], op=ALU.add)
        if S1PART == 3:
            continue
        pt = ptb[k]
        P.transpose(pt[:, 0, :], kb[k][:, 0:128], identb, signal=False)
        P.transpose(pt[:, 1, :], kb[k][:, 128:256], identb, signal=True)
        P.op("act", "copy", out=kvnT[:, t * 128:(t + 1) * 128], in_=pt[:, 0, :])
        P.op("act", "copy", out=kpeT[0:32, t * 128:(t + 1) * 128], in_=pt[0:32, 1, :])
    for s in range(2):
        P.dma("sp", kT[s][64:96, :], kpeT[0:32, :])
    if STOP == 1:
        P.barrier(); es.close(); return

    es2 = es
    NG = NH // HG
    NCOL = HG * QK
    NHF = NCOL // 384
    hq = P.sbn("hq", [128, 256], F32, 3, es=es2)
    qn = P.sbn("qn", [128, 256], BF16, 2, es=es2)
    qnT = P.sbn("qnT", [128, 2, 128], BF16, 2, es=es2)
    qf = P.sbn("qf", [128, HG, QK], F32, 2, es=es2)
    qa = P.sbn("qa", [128, HG, 128], BF16, 2, es=es2)
    ss = P.sbn("ssq", [128, 4], F32, 2, es=es2)
    rs = P.sbn("rsq", [128, 4], F32, 2, es=es2)
    t1 = P.sbn("t1q", [128, HG, 16], F32, 2, es=es2)
    t2 = P.sbn("t2q", [128, HG, 16], F32, 2, es=es2)
    for k in range(2):
        P.op("pool", "memset", ap=qa[k].ap, constant=0.0, writes=[qa[k]])
    if sel is not None:
        mk = load_mask(P, es, sel[0])
        hq2 = P.sbn("hq2", [128, 256], F32, 2, es=es2)
    rope_tables_all(P, es, pos_q, NQT, invfB, cs_all, sn_all, "q")
    pT = P.sbn("pT", [128, 2, 512], BF16, 3, es=es)
    oT = P.sbn("oT", [128, 512], F32, 2, es=es)
    for k in range(2):
        P.op("pool", "memset", ap=oT[k].ap, constant=0.0, writes=[oT[k]])
    yo = P.sbn("yo", [128, 4, 64], F32, 2, es=es)
    rc = P.sbn("rc", [128, 4], F32, 2, es=es)
    ci = 0

    def loadq(t):
        if t < NQT:
            P.dma("sp", hq[t % 3], h_q[t * 128:(t + 1) * 128, :])
            if sel is not None:
                P.dma("sp", hq2[t % 2], sel[1][t * 128:(t + 1) * 128, :])
                blend(P, hq[t % 3], hq2[t % 2], mk)

    for hg in range(NG):
        loadq(0)
        loadq(1)
        for t in range(NQT):
            loadq(t + 2)
            k = t % 2
            x = hq[t % 3]
            pt = ptb[k]
            norm_transpose(P, x, gq, qn[k], None, ss[k], rs[k], pt, qnT[k], identb, nd=2)
            for hf in range(NHF):
                po = pss[(t * 2 + hf) % 4]
                c0 = hg * NCOL + hf * 384
                for c in range(2):
                    P.mm(po[:, 0:384], qnT[k][:, c, :], Wq[c][:, c0:c0 + 384], c == 0, c == 1)
                P.op("act", "copy", out=qf[k][:, hf * 4:(hf + 1) * 4, :], in_=po[:, 0:384].re("p (h d) -> p h d", h=4))
            P.op("act", "copy", out=qa[k][:, :, 0:64], in_=qf[k][:, :, 0:64])
            x1 = qf[k][:, :, 64:80]
            x2 = qf[k][:, :, 80:96]
            csb = cs_all[:, t, :].re("p (o d) -> p o d", o=1).bc([128, HG, 16])
            snb = sn_all[:, t, :].re("p (o d) -> p o d", o=1).bc([128, HG, 16])
            P.op("dve", "tensor_tensor", out=t1[k], in0=x1, in1=csb, op=ALU.mult)
            P.op("dve", "tensor_tensor", out=t2[k], in0=x2, in1=snb, op=ALU.mult)
            P.op("dve", "tensor_tensor", out=qa[k][:, :, 64:80], in0=t1[k], in1=t2[k], op=ALU.subtract)
            P.op("dve", "tensor_tensor", out=t1[k], in0=x1, in1=snb, op=ALU.mult)
            P.op("dve", "tensor_tensor", out=t2[k], in0=x2, in1=csb, op=ALU.mult)
            P.op("dve", "tensor_tensor", out=qa[k][:, :, 80:96], in0=t1[k], in1=t2[k], op=ALU.add)
            pt2 = ptb[(k + 1) % 2]
            for hl in range(HG):
                P.transpose(pt2[:, hl, :], qa[k][:, hl, :], identb, signal=(hl == HG - 1))
            P.op("act", "copy", out=qT[0:97, :, t * 128:(t + 1) * 128], in_=pt2[0:97, 0:HG, :])
        for hl in range(HG):
            h = hg * HG + hl
            s = h % 2
            for kb_ in range(NK // 512):
                pp = pss[kb_ % 4]
                P.mm(pp[0:64, :], Wkv[:, h * 128:h * 128 + 64], kvnT[:, kb_ * 512:(kb_ + 1) * 512], True, True)
                if kb_ % 2 == 0:
                    P.op("act", "copy", out=kT[s][0:64, kb_ * 512:(kb_ + 1) * 512], in_=pp[0:64, :])
                else:
                    P.op("dve", "tensor_copy", out=kT[s][0:64, kb_ * 512:(kb_ + 1) * 512], in_=pp[0:64, :])
            for g in range(NKT // 8):
                pp = pss[g % 4]
                for j in range(8):
                    kt = g * 8 + j
                    P.mm(pp[:, j * 64:(j + 1) * 64], kvnT[:, kt * 128:(kt + 1) * 128],
                         Wkv[:, h * 128 + 64:h * 128 + 128], True, True)
                src = pp.re("p (j d) -> p j d", j=8)
                if g % 2 == 0:
                    P.op("act", "copy", out=vA[s][:, g * 8:(g + 1) * 8, 0:64], in_=src)
                else:
                    P.op("dve", "tensor_copy", out=vA[s][:, g * 8:(g + 1) * 8, 0:64], in_=src)
            for qb in range(NQB):
                po = pso[0]
                rhs_q = qT[0:97, hl, qb * 512:(qb + 1) * 512]
                NP2 = NKT // 2

                def S(j):
                    for u_ in range(2):
                        kt = 2 * j + u_
                        P.mm(pss2[j % 2][:, u_, :], kT[s][0:97, kt * 128:(kt + 1) * 128], rhs_q, True, True)

                S(0)
                for j in range(NP2):
                    if j + 1 < NP2:
                        S(j + 1)
                    pb = pT[ci % 3]
                    ci += 1
                    P.op("act", "activation", out=pb, in_=pss2[j % 2], func=AF.Exp, scale=SC)
                    for u_ in range(2):
                        P.mm(po[0:65, :], vA[s][:, 2 * j + u_, :], pb[:, u_, :], (j == 0 and u_ == 0),
                             (j == NP2 - 1 and u_ == 1))
                k2 = qb % 2
                P.op("dve", "tensor_copy", out=oT[k2][0:65, :], in_=po[0:65, :])
                for j in range(4):
                    P.transpose(pmisc[:, j * 128:(j + 1) * 128], oT[k2][:, j * 128:(j + 1) * 128], identf,
                                signal=(j == 3))
                pv = pmisc.re("p (j d) -> p j d", j=4)
                P.op("dve", "reciprocal", out=rc[k2].re("p (j o) -> p j o", o=1), in_=pv[:, :, 64:65])
                P.op("dve", "tensor_tensor", out=yo[k2], in0=pv[:, :, 0:64],
                     in1=rc[k2].re("p (j o) -> p j o", o=1).bc([128, 4, 64]), op=ALU.mult)
                P.dma("pool", y_out[qb * 512:(qb + 1) * 512, h * 64:(h + 1) * 64].rearrange("(j p) d -> p j d", p=128),
                      yo[k2], dram_out=True)
    P.barrier()
    es.close()
CK = 31
HALO = 15


def stage_conv(P, hcT, dw_w, dw_b, ln_g, ln_b, y_out, NT, sel=None):
    es = contextlib.ExitStack()
    NP = NT + 2 * HALO
    identf = make_ident(P, es, F32, "identf")
    wT = P.sb("cwT", [128, 2, CK], F32, es=es)
    for c in range(2):
        P.dma("sp", wT[:, c, :], dw_w[:, c * 128:(c + 1) * 128].rearrange("j c -> c j"), allow_slow_non_contiguous=True)
    bT = P.sb("cbT", [128, 2], F32, es=es)
    P.dma("sp", bT, dw_b.rearrange("(c p) -> p c", p=128), allow_slow_non_contiguous=True)
    gB = P.sb("clg", [128, 256], F32, es=es)
    P.dma("sp", gB, ln_g.partition_broadcast(128))
    bB = P.sb("clb", [128, 256], F32, es=es)
    P.dma("sp", bB, ln_b.partition_broadcast(128))
    diag = P.sbn("cdiag", [128, CK, 128], BF16, 2, es=es)
    for c in range(2):
        for j in range(CK):
            P.op("dve", "tensor_scalar", out=diag[c][:, j, :], in0=identf, scalar1=wT[:, c, j:j + 1], scalar2=None,
                 op0=ALU.mult)
    u = P.sbn("cu", [128, NP], BF16, 2, es=es)
    a_t = P.sbn("ca", [128, 1024], F32, 2, es=es)
    g_t = P.sbn("cg", [128, 1024], F32, 2, es=es)
    if sel is not None:
        mk = load_mask(P, es, sel[0])
        a2 = P.sbn("ca2", [128, 1024], F32, 2, es=es)
        g2 = P.sbn("cg2", [128, 1024], F32, 2, es=es)
    i = 0
    for c in range(2):
        for o in range(0, NP, 1024):
            w = min(1024, NP - o)
            k = i % 2
            i += 1
            P.dma("sp", a_t[k][:, :w], hcT[c * 128:(c + 1) * 128, o:o + w])
            P.dma("sp", g_t[k][:, :w], hcT[256 + c * 128:256 + (c + 1) * 128, o:o + w])
            if sel is not None:
                P.dma("sp", a2[k][:, :w], sel[1][c * 128:(c + 1) * 128, o:o + w])
                P.dma("sp", g2[k][:, :w], sel[1][256 + c * 128:256 + (c + 1) * 128, o:o + w])
                blend(P, a_t[k][:, :w], a2[k][:, :w], mk)
                blend(P, g_t[k][:, :w], g2[k][:, :w], mk)
            P.op("act", "activation", out=g_t[k][:, :w], in_=g_t[k][:, :w], func=AF.Sigmoid)
            P.op("dve", "tensor_tensor", out=u[c][:, o:o + w], in0=a_t[k][:, :w], in1=g_t[k][:, :w], op=ALU.mult)
    pc = [P.ps("pc%d" % i, [128, 512], F32, es=es) for i in range(2)]
    ptk = [P.ps("ptk%d" % i, [128, 256], F32, es=es) for i in range(2)]
    uc = P.sbn("cuc", [128, 512], F32, 4, es=es)
    xc = P.sbn("cxc", [128, 256], F32, 2, es=es)
    jk = P.sbn("cjk", [128, 256], F32, 2, es=es)
    st = P.sbn("cst", [128, 8], F32, 2, es=es)
    yo = P.sbn("cyo", [128, 256], F32, 2, es=es)
    n = 0
    for b in range(NT // 512):
        for c in range(2):
            p = pc[c]
            for j in range(CK):
                P.mm(p, diag[c][:, j, :], u[c][:, b * 512 + j: b * 512 + j + 512], j == 0, j == CK - 1)
            P.op("act", "activation", out=uc[(b % 2) * 2 + c], in_=p, func=AF.Identity, bias=bT[:, c:c + 1], scale=1.0)
        for tt in range(4):
            k = n % 2
            n += 1
            pt = ptk[k]
            for c in range(2):
                P.transpose(pt[:, c * 128:(c + 1) * 128], uc[(b % 2) * 2 + c][:, tt * 128:(tt + 1) * 128], identf,
                            signal=(c == 1))
            s = st[k]
            P.op("act", "activation", out=xc[k], in_=pt, func=AF.Identity, accum_out=s[:, 0:1])
            P.op("act", "activation", out=jk[k], in_=pt, func=AF.Square, accum_out=s[:, 1:2])
            P.op("dve", "tensor_scalar", out=s[:, 2:3], in0=s[:, 0:1], scalar1=1.0 / 256, scalar2=None, op0=ALU.mult)
            P.op("dve", "tensor_tensor", out=s[:, 3:4], in0=s[:, 2:3], in1=s[:, 2:3], op=ALU.mult)
            P.op("dve", "scalar_tensor_tensor", out=s[:, 4:5], in0=s[:, 1:2], scalar=1.0 / 256, in1=s[:, 3:4],
                 op0=ALU.mult, op1=ALU.subtract)
            rms_scale(P, s[:, 4:5], s[:, 5:6])
            P.op("dve", "tensor_scalar", out=xc[k], in0=xc[k], scalar1=s[:, 2:3], scalar2=s[:, 5:6],
                 op0=ALU.subtract, op1=ALU.mult)
            P.op("dve", "tensor_tensor", out=xc[k], in0=xc[k], in1=gB, op=ALU.mult)
            P.op("dve", "tensor_tensor", out=xc[k], in0=xc[k], in1=bB, op=ALU.add)
            P.op("act", "activation", out=yo[k], in_=xc[k], func=AF.Silu)
            t0 = b * 512 + tt * 128
            P.dma("pool", y_out[t0:t0 + 128, :], yo[k], dram_out=True)
    P.barrier()
    es.close()
LSEQ = 8192
NFFT = 16384


def hyena_consts():
    n = np.arange(128)
    ang = 2 * np.pi * np.outer(n, n) / 128.0
    C = np.cos(ang)
    S = np.sin(ang)
    angT = 2 * np.pi * np.outer(n, n) / float(NFFT)
    out = {
        "c_dft": np.concatenate([C, S, -S, -C], 1).astype(np.float32),
        "c_tw": np.concatenate([np.cos(angT), np.sin(angT)], 1).astype(np.float32),
    }
    L = LSEQ
    t = np.linspace(0.0, 1.0, L, dtype=np.float32)[:, None]
    bands = 16
    a = (2.0 * np.pi * np.arange(L, dtype=np.float32)[:, None] / L).astype(np.float32)
    fb = np.linspace(1e-4, bands - 1, bands, dtype=np.float32)[None, :]
    z = np.concatenate([t, np.cos(fb * a), -np.sin(fb * a)], -1).astype(np.float32)
    out["c_zT"] = np.ascontiguousarray(z.T)
    max_decay = math.log(1e-2) / 0.3
    min_decay = math.log(1e-2) / 1.5
    deltas = np.linspace(min_decay, max_decay, 256, dtype=np.float32)
    dec = np.exp(-t * np.abs(deltas)[None, :]).astype(np.float32)
    d4 = dec.reshape(64, 128, 2, 128).transpose(2, 1, 0, 3)
    out["c_dec"] = np.ascontiguousarray(d4)
    return out


def stage_hyena(P, hhT, sw, sb_, w1, b1, w2, b2, w3, b3, w4, freq, bias_d, c_dft, c_tw, c_zT, c_dec,
                yT, scr_u, scr_z, scr_k, STOPH=0):
    es = contextlib.ExitStack()
    L = LSEQ
    Cm = P.sb("Cm", [128, 128], BF16, es=es)
    Sm = P.sb("Sm", [128, 128], BF16, es=es)
    nSm = P.sb("nSm", [128, 128], BF16, es=es)
    nCm = P.sb("nCm", [128, 128], BF16, es=es)
    CSn = P.sb("CSn", [128, 256], BF16, es=es)
    CS = P.sb("CS", [128, 256], BF16, es=es)
    SC2 = P.sb("SC2", [128, 256], BF16, es=es)
    P.dma("pool", Cm, c_dft[:, 0:128])
    P.dma("pool", Sm, c_dft[:, 128:256])
    P.dma("pool", nSm, c_dft[:, 256:384])
    P.dma("pool", nCm, c_dft[:, 384:512])
    P.dma("pool", CSn[:, 0:128], c_dft[:, 0:128])
    P.dma("pool", CSn[:, 128:256], c_dft[:, 256:384])
    P.dma("pool", CS, c_dft[:, 0:256])
    nCS = P.sb("nCS", [128, 256], BF16, es=es)
    P.dma("pool", nCS[:, 0:128], c_dft[:, 384:512])
    P.dma("pool", nCS[:, 128:256], c_dft[:, 256:384])
    P.dma("pool", SC2[:, 0:128], c_dft[:, 256:384])
    P.dma("pool", SC2[:, 128:256], c_dft[:, 0:128])
    Tw = P.sb("Tw", [128, 256], F32, es=es)
    P.dma("sp", Tw, c_tw)
    Twb = P.sb("Twb", [128, 256], BF16, es=es)
    P.op("act", "copy", out=Twb, in_=Tw)
    Tc = Twb[:, 0:128].re("p (o k) -> p o k", o=1).bc([128, 4, 128])
    Ts = Twb[:, 128:256].re("p (o k) -> p o k", o=1).bc([128, 4, 128])
    Asb = P.sbn("Asb", [128, 4, 2, 128], BF16, 2, es=es)
    Xsb = P.sbn("Xsb", [128, 2, 4, 128], BF16, 2, es=es)
    evn = [0]
    ones = P.sb("ones", [128, 128], F32, es=es)
    P.op("pool", "memset", ap=ones.ap, constant=1.0, writes=[ones])
    pA = P.ps("pA", [128, 4, 2, 128], F32, es=es)
    pXr = P.ps("pXr", [128, 4, 128], F32, es=es)
    pXi = P.ps("pXi", [128, 4, 128], F32, es=es)
    pG = P.ps("pG", [128, 4, 2, 128], F32, es=es)
    pY = P.ps("pY", [128, 4, 128], F32, es=es)
    pM = P.ps("pM", [128, 512], F32, es=es)
    QA = [[P.sb("QA%d_%d" % (k, i), [128, 4, 128], BF16, es=es) for i in range(4)] for k in range(2)]

    def prods(Q, Ar, Ai, Wr, Wi):
        P.op("dve", "tensor_tensor", out=Q[0], in0=Ar, in1=Wr, op=ALU.mult)
        P.op("dve", "tensor_tensor", out=Q[1], in0=Ai, in1=Wi, op=ALU.mult)
        P.op("dve", "tensor_tensor", out=Q[2], in0=Ai, in1=Wr, op=ALU.mult)
        P.op("dve", "tensor_tensor", out=Q[3], in0=Ar, in1=Wi, op=ALU.mult)

    def fwd_s1(xb, pa):
        for ch in range(4):
            P.mm(pa[:, ch].re("p a k -> p (a k)"), xb[:, ch, :], CSn[0:64, :], True, True)

    def fwd_s2(k, pa):
        a = Asb[evn[0] % 2]
        evn[0] += 1
        P.op("act", "copy", out=a, in_=pa)
        prods(QA[k], a[:, :, 0, :], a[:, :, 1, :], Tc, Ts)

    def fwd_stage3(k, first, last, conj_part=False, px=None):
        q = [QA[k][i].re("p c k -> p (c k)") for i in range(4)]
        pxr, pxi = px if px is not None else (pXr, pXi)
        xr = pxr.re("p c k -> p (c k)")
        xi = pxi.re("p c k -> p (c k)")
        P.mm(xr, Cm, q[0], first, False)
        P.mm(xr, Cm, q[1], False, False)
        P.mm(xr, Sm, q[2], False, False)
        P.mm(xr, nSm, q[3], False, last)
        if not conj_part:
            P.mm(xi, Cm, q[2], first, False)
            P.mm(xi, nCm, q[3], False, False)
            P.mm(xi, nSm, q[0], False, False)
            P.mm(xi, nSm, q[1], False, last)
        else:
            P.mm(xi, nCm, q[2], first, False)
            P.mm(xi, Cm, q[3], False, False)
            P.mm(xi, Sm, q[0], False, False)
            P.mm(xi, Sm, q[1], False, last)

    swT = P.sb("swT", [128, 3, 3], F32, es=es)
    P.dma("sp", swT, sw.rearrange("a j c -> c a j"), allow_slow_non_contiguous=True)
    sbT = P.sb("sbT", [128, 3], F32, es=es)
    P.dma("sp", sbT, sb_.rearrange("a c -> c a"), allow_slow_non_contiguous=True)
    es0 = contextlib.ExitStack()
    hin = P.sbn("hin", [128, 2050], F32, 2, es=es0)
    hou = P.sbn("hou", [128, 2048], F32, 2, es=es0)
    i = 0
    for a in range(3):
        for blk in range(L // 2048):
            k = i % 2
            i += 1
            P.dma("sp", hin[k], hhT[a][:, blk * 2048: blk * 2048 + 2050])
            P.op("dve", "tensor_scalar", out=hou[k], in0=hin[k][:, 0:2048], scalar1=swT[:, a, 0:1], scalar2=sbT[:, a:a + 1],
                 op0=ALU.mult, op1=ALU.add)
            P.op("dve", "scalar_tensor_tensor", out=hou[k], in0=hin[k][:, 1:2049], scalar=swT[:, a, 1:2], in1=hou[k],
                 op0=ALU.mult, op1=ALU.add)
            P.op("dve", "scalar_tensor_tensor", out=hou[k], in0=hin[k][:, 2:2050], scalar=swT[:, a, 2:3], in1=hou[k],
                 op0=ALU.mult, op1=ALU.add)
            P.dma("pool", scr_u[a, :, blk * 2048:(blk + 1) * 2048], hou[k])
    P.barrier()
    es0.close()
    if STOPH == 1:
        P.barrier(); es.close(); return

    sinv = P.sb("sinv", [128, 2, 128], F32, es=es)
    esf = contextlib.ExitStack()
    w1t = P.sb("w1t", [33, 64], F32, es=esf)
    P.dma("sp", w1t, w1)
    w2t = P.sb("w2t", [64, 64], F32, es=esf)
    P.dma("sp", w2t, w2)
    w3t = P.sb("w3t", [64, 64], F32, es=esf)
    P.dma("sp", w3t, w3)
    w4t = P.sb("w4t", [64, 4, 128], F32, es=esf)
    P.dma("sp", w4t, w4)
    fr = P.sb("fr", [64, 1], F32, es=esf)
    P.dma("sp", fr, freq.rearrange("(p o) -> p o", o=1))
    bb = P.sb("bb", [64, 3], F32, es=esf)
    for j, bj in enumerate((b1, b2, b3)):
        P.dma("sp", bb[:, j:j + 1], bj.rearrange("(p o) -> p o", o=1))
    fbb = P.sb("fbb", [64, 3], F32, es=esf)
    P.op("dve", "tensor_scalar", out=fbb, in0=bb, scalar1=fr, scalar2=None, op0=ALU.mult)
    zT = P.sb("zT", [33, L], F32, es=esf)
    for q in range(4):
        P.dma("sp", zT[:, q * 2048:(q + 1) * 2048], c_zT[:, q * 2048:(q + 1) * 2048])
    hA = P.sb("hA", [64, L], F32, es=esf)
    hB = P.sb("hB", [64, L], F32, es=esf)
    arg = P.sbn("farg", [64, 512], F32, 2, es=esf)
    ti = P.sbn("fti", [64, 512], I32, 2, es=esf)
    tf = P.sbn("ftf", [64, 512], F32, 2, es=esf)
    srcs = [(zT, w1t, 33), (hA, w2t, 64), (hB, w3t, 64)]
    dsts = [hA, hB, hA]
    n = 0
    for li in range(3):
        src, wt, kk = srcs[li]
        dst = dsts[li]
        for blk in range(L // 512):
            k = n % 2
            n += 1
            P.mm(pM[0:64, :], wt[0:kk, :], src[0:kk, blk * 512:(blk + 1) * 512], True, True)
            P.op("act", "activation", out=arg[k], in_=pM[0:64, :], func=AF.Identity, bias=fbb[:, li:li + 1], scale=fr)
            sin_rr(P, dst[:, blk * 512:(blk + 1) * 512], arg[k], ti[k], tf[k], 0.0)
    h3 = hA
    kf = P.sbn("kf", [64, 128, 128], BF16, 2, es=esf)
    dec = P.sbn("dec", [64, 128], F32, 3, es=esf)
    l1p = P.sb("l1p", [64, 2, 128], F32, es=esf)
    ksr = P.sbn("ksr", [128, 4, 128], BF16, 2, es=esf)
    ksi = P.sbn("ksi", [128, 4, 128], BF16, 2, es=esf)
    h3v = h3.re("p (n1 n2) -> p n2 n1", n2=128)
    for o in range(2):
        for n2 in range(128):
            k = n2 % 3
            P.dma("sp", dec[k], c_dec[n2])
            pr = [pM, pY.re("p c k -> p (c k)"), pXr.re("p c k -> p (c k)"), pXi.re("p c k -> p (c k)")][n2 % 4]
            P.mm(pr[0:64, 0:256], h3v[:, n2, :], w4t[:, 2 * o:2 * o + 2, :].re("p d c -> p (d c)"), True, True)
            for d in range(2):
                eng = "dve"
                P.op(eng, "tensor_tensor", out=kf[d][:, :, n2], in0=pr[0:64, d * 128:(d + 1) * 128], in1=dec[k],
                     op=ALU.mult)
        P.op("pool", "memset", ap=kf[1][0:1, :, 0:1].ap, constant=0.0, writes=[kf[1]])
        for d in range(2):
            P.op("dve", "tensor_reduce", out=l1p[:, d, :], in_=kf[d], axis=AX.X, op=ALU.add, apply_absolute_value=True)
        P.op("dve", "tensor_tensor", out=l1p[:, 0, :], in0=l1p[:, 0, :], in1=l1p[:, 1, :], op=ALU.add)
        P.mm(pM[:, 256:384], ones[0:64, :], l1p[:, 0, :], True, True)
        P.op("dve", "tensor_scalar", out=sinv[:, o, :], in0=pM[:, 256:384], scalar1=float(NFFT), scalar2=None, op0=ALU.mult)
        P.op("dve", "reciprocal", out=sinv[:, o, :], in_=sinv[:, o, :])
        pas = [pA, pG]
        pxs = [(pXr, pXi), (pY, pM.re("p (c k) -> p c k", c=4))]
        NTF = 64
        fwd_s1(kf[0][:, 0:4, :], pas[0])
        for n in range(NTF):
            g, d = n // 2, n % 2
            if n + 1 < NTF:
                g1, d1 = (n + 1) // 2, (n + 1) % 2
                fwd_s1(kf[d1][:, g1 * 4:(g1 + 1) * 4, :], pas[(n + 1) % 2])
            fwd_s2(n % 2, pas[n % 2])
            px = pxs[g % 2]
            fwd_stage3(n % 2, d == 0, d == 1, conj_part=(d == 1), px=px)
            if d == 1:
                k = g % 2
                P.op("act", "copy", out=ksr[k], in_=px[0])
                P.op("act", "copy", out=ksi[k], in_=px[1])
                P.dma("pool", scr_k[o, g, 0], ksr[k])
                P.dma("pool", scr_k[o, g, 1], ksi[k])
    P.barrier()
    esf.close()
    if STOPH == 2:
        P.barrier(); es.close(); return

    QB = [[P.sb("QB%d_%d" % (k, i), [128, 4, 128], BF16, es=es) for i in range(4)] for k in range(2)]
    QC = [[P.sb("QC%d_%d" % (k, i), [128, 4, 128], BF16, es=es) for i in range(4)] for k in range(2)]
    dB = P.sb("dB", [64, 2, 128], F32, es=es)
    P.dma("sp", dB, bias_d.rearrange("o c -> (o c)").partition_broadcast(64).rearrange("p (o c) -> p o c", o=2))
    dsB = P.sb("dsB", [64, 2, 128], F32, es=es)
    rsv = P.sb("rsv", [64, 2, 128], F32, es=es)
    P.op("dve", "reciprocal", out=rsv, in_=sinv[0:64])
    P.op("dve", "tensor_tensor", out=dsB, in0=dB, in1=rsv, op=ALU.mult)
    xin = P.sbn("xin", [64, 4, 128], F32, 2, es=es)
    xg = P.sbn("xg", [64, 4, 128], F32, 2, es=es)
    xb = P.sbn("xb", [64, 4, 128], BF16, 2, es=es)
    Kr = P.sbn("Kr", [128, 4, 128], BF16, 2, es=es)
    Ki = P.sbn("Ki", [128, 4, 128], BF16, 2, es=es)
    dv = P.sbn("dv", [64, 4, 128], F32, 2, es=es)
    zo = P.sbn("zo", [64, 4, 128], F32, 2, es=es)
    scr_zv = View(scr_z, [Buf("scr_z")])

    def front(o, g):
        k = g % 2
        cs4 = slice(g * 4, (g + 1) * 4)
        if o == 0:
            P.dma("sp", xin[k], scr_u[0, cs4, :].re("c (n1 n2) -> n1 c n2", n2=128))
            P.dma("sp", xg[k], scr_u[1, cs4, :].re("c (n1 n2) -> n1 c n2", n2=128))
        else:
            P.dma("sp", xin[k], scr_zv[g])
            P.dma("sp", xg[k], scr_u[2, cs4, :].re("c (n1 n2) -> n1 c n2", n2=128))
        P.dma("sp", Kr[k], scr_k[o, g, 0])
        P.dma("sp", Ki[k], scr_k[o, g, 1])
        P.op("act", "copy", out=xb[k], in_=xin[k])
        fwd_s1(xb[k], pA)
        fwd_s2(k, pA)
        fwd_stage3(k, True, True)
        xs = Xsb[k]
        P.op("act", "copy", out=xs[:, 0], in_=pXr)
        P.op("act", "copy", out=xs[:, 1], in_=pXi)
        prods(QB[k], xs[:, 0], xs[:, 1], Kr[k], Ki[k])

    def back(o, g):
        k = g % 2
        cs4 = slice(g * 4, (g + 1) * 4)
        for ch in range(4):
            gv = pG[:, ch].re("p a k -> p (a k)")
            P.mm(gv, QB[k][0][:, ch, :], CS, True, False)
            P.mm(gv, QB[k][1][:, ch, :], nCS, False, False)
            P.mm(gv, QB[k][2][:, ch, :], SC2, False, False)
            P.mm(gv, QB[k][3][:, ch, :], SC2, False, True)
        a = Asb[evn[0] % 2]
        evn[0] += 1
        P.op("act", "copy", out=a, in_=pG)
        prods(QC[k], a[:, :, 0, :], a[:, :, 1, :], Tc, Ts)
        yv = pY[0:64].re("p c k -> p (c k)")
        qc = [QC[k][i].re("p c k -> p (c k)") for i in range(4)]
        P.mm(yv, Cm[:, 0:64], qc[0], True, False)
        P.mm(yv, nCm[:, 0:64], qc[1], False, False)
        P.mm(yv, nSm[:, 0:64], qc[2], False, False)
        P.mm(yv, nSm[:, 0:64], qc[3], False, True)
        for ch in range(4):
            c = g * 4 + ch
            P.op("act", "activation", out=dv[k][:, ch, :], in_=xin[k][:, ch, :], func=AF.Identity,
                 scale=dsB[:, o, c:c + 1])
            P.op("act", "activation", out=xg[k][:, ch, :], in_=xg[k][:, ch, :], func=AF.Identity,
                 scale=sinv[0:64, o, c:c + 1])
        P.op("dve", "tensor_tensor", out=dv[k], in0=pY[0:64], in1=dv[k], op=ALU.add)
        P.op("pool", "tensor_tensor", out=zo[k], in0=dv[k], in1=xg[k], op=ALU.mult)
        if o == 0:
            P.dma("pool", scr_zv[g], zo[k])
        else:
            P.dma("pool", yT[cs4, :].rearrange("c (n1 n2) -> n1 c n2", n2=128), zo[k], dram_out=True)

    for o in range(2):
        front(o, 0)
        for g in range(32):
            if g + 1 < 32:
                front(o, g + 1)
            back(o, g)
    P.barrier()
    es.close()
from concourse.bass_utils import run_bass_kernel_spmd

NCORES = 8
BATCH = 4
SEQ = 8192
DEPTH = 2
PADW = SEQ + 30
_PROG_CACHE = {}

_WNAMES = [("ffn1_norm", [1024]), ("ffn1_w_gate", [1024, 2816]), ("ffn1_w_up", [1024, 2816]), ("ffn1_w_down", [2816, 1024]),
           ("mix_norm", [1024]), ("w_in", [1024, 1696]), ("mla_q_norm", [256]), ("mla_w_qb", [256, 768]),
           ("mla_kv_norm", [128]), ("mla_w_kvb", [128, 1024]), ("conv_dw_w", [31, 256]), ("conv_dw_b", [256]),
           ("conv_ln_g", [256]), ("conv_ln_b", [256]), ("hy_filt_w1", [33, 64]), ("hy_filt_b1", [64]),
           ("hy_filt_w2", [64, 64]), ("hy_filt_b2", [64]), ("hy_filt_w3", [64, 64]), ("hy_filt_b3", [64]),
           ("hy_filt_freq", [64]), ("out_norm", [1024]), ("w_out", [1024, 1024]), ("ffn2_norm", [1024]),
           ("ffn2_w_gate", [1024, 2816]), ("ffn2_w_up", [1024, 2816]), ("ffn2_w_down", [2816, 1024])]


def _dt(nc, name, shape, kind="ExternalInput", dtype=F32):
    return nc.dram_tensor(name, list(shape), dtype, kind=kind).ap()


def build_fused():
    nc = bass.Bass("TRN2", target_bir_lowering=False)
    L = SEQ
    x = _dt(nc, "x", [L, 1024])
    pos = _dt(nc, "pos", [L, 1], dtype=I32)
    invf = _dt(nc, "invf", [16])
    Wt = {}
    for nm, shp in _WNAMES:
        Wt[nm] = _dt(nc, nm, [DEPTH] + shp)
    gf = _dt(nc, "final_norm", [1024])
    hsw = _dt(nc, "hsw", [DEPTH, 2, 3, 3, 128]); hsb = _dt(nc, "hsb", [DEPTH, 2, 3, 128])
    hw4 = _dt(nc, "hw4", [DEPTH, 2, 64, 4, 128]); hbd = _dt(nc, "hbd", [DEPTH, 2, 2, 128])
    c_dft = _dt(nc, "c_dft", [128, 512]); c_tw = _dt(nc, "c_tw", [128, 256]); c_zT = _dt(nc, "c_zT", [33, L])
    c_dec = _dt(nc, "c_dec", [2, 128, 64, 128])
    H = L // 2
    out = _dt(nc, "out", [H, 1024], "ExternalOutput")
    mk = _dt(nc, "mk", [2])
    posq = _dt(nc, "posq", [H, 1], dtype=I32)
    xa = _dt(nc, "s_xa", [L, 1024], "Internal"); xb = _dt(nc, "s_xb", [L, 1024], "Internal")
    htm = _dt(nc, "s_htm", [L, 416], "Internal"); hfm = _dt(nc, "s_hfm", [1280, PADW], "Internal")
    ymla = _dt(nc, "s_ymla", [L, 512], "Internal"); yconv = _dt(nc, "s_yconv", [L, 256], "Internal")
    yhT = _dt(nc, "s_yhT", [256, L], "Internal")
    scr_u = _dt(nc, "scr_u", [3, 128, L], "Internal"); scr_z = _dt(nc, "scr_z", [32, 64, 4, 128], "Internal")
    scr_k = _dt(nc, "scr_k", [2, 32, 2, 128, 4, 128], "Internal", dtype=BF16)
    P = Prog(nc)
    es = contextlib.ExitStack()
    zt = P.sb("zt", [128, 15], F32, es=es)
    P.op("pool", "memset", ap=zt.ap, constant=0.0, writes=[zt])
    for r in range(10):
        P.dma("sp", hfm[r * 128:(r + 1) * 128, 0:15], zt)
        P.dma("sp", hfm[r * 128:(r + 1) * 128, 15 + L:30 + L], zt)
    P.barrier()
    es.close()
    cur = x
    for i in range(DEPTH):
        last = (i == DEPTH - 1)
        w = lambda nm: Wt[nm][i]
        stage_ffn(P, cur, xa, w("ffn1_norm"), w("ffn1_w_gate"), w("ffn1_w_up"), w("ffn1_w_down"), L)
        stage_win(P, xa, w("mix_norm"), w("w_in"), htm, hfm[:, 15:15 + L], L)
        if not last:
            stage_conv(P, hfm[0:512, :], w("conv_dw_w"), w("conv_dw_b"), w("conv_ln_g"), w("conv_ln_b"), yconv, L)
            stage_mla(P, htm[:, 0:256], htm[:, 256:416], pos, pos, invf,
                      w("mla_q_norm"), w("mla_w_qb"), w("mla_kv_norm"), w("mla_w_kvb"), ymla, L, L, HG=4)
        else:
            stage_conv(P, hfm[0:512, 0:H + 30], w("conv_dw_w"), w("conv_dw_b"), w("conv_ln_g"), w("conv_ln_b"),
                       yconv[0:H, :], H, sel=(mk, hfm[0:512, H:H + H + 30]))
            stage_mla(P, htm[0:H, 0:256], htm[:, 256:416], pos, posq, invf,
                      w("mla_q_norm"), w("mla_w_qb"), w("mla_kv_norm"), w("mla_w_kvb"), ymla[0:H, :], H, L,
                      sel=(mk, htm[H:L, 0:256]))
        for hc in range(2):
            hh = [hfm[512 + a * 256 + hc * 128: 512 + a * 256 + hc * 128 + 128, 14:14 + L + 2] for a in range(3)]
            stage_hyena(P, hh, hsw[i, hc], hsb[i, hc], w("hy_filt_w1"), w("hy_filt_b1"), w("hy_filt_w2"), w("hy_filt_b2"),
                        w("hy_filt_w3"), w("hy_filt_b3"), hw4[i, hc], w("hy_filt_freq"), hbd[i, hc], c_dft, c_tw, c_zT,
                        c_dec[hc], yhT[hc * 128:(hc + 1) * 128, :], View(scr_u, [Buf("scr_u")]), scr_z,
                        View(scr_k, [Buf("scr_k")]))
        if not last:
            stage_wout(P, xa, xb, [(ymla, 0, 512), (yconv, 512, 256)], w("out_norm"), w("w_out"), L, y_cm=(yhT, 768, 256))
            stage_ffn(P, xb, xa, w("ffn2_norm"), w("ffn2_w_gate"), w("ffn2_w_up"), w("ffn2_w_down"), L)
            cur = xa
        else:
            stage_wout(P, xa, xb[0:H, :], [(ymla[0:H, :], 0, 512), (yconv[0:H, :], 512, 256)], w("out_norm"), w("w_out"), H,
                       y_cm=(yhT, 768, 256), sel=(mk, H))
            stage_ffn(P, xb[0:H, :], out, w("ffn2_norm"), w("ffn2_w_gate"), w("ffn2_w_up"), w("ffn2_w_down"), H, g_final=gf)
    P.finish(); P.close()
    return nc


def kernel(**inp):
    f32 = lambda a: np.ascontiguousarray(np.asarray(a, dtype=np.float32))
    x = f32(inp["x"])
    pos = np.ascontiguousarray(np.asarray(inp["positions"]).astype(np.int32))
    cst = hyena_consts()
    invf = (1.0 / (10000.0 ** (np.arange(0, 32, 2, dtype=np.float32) / 32))).astype(np.float32)
    shared = {nm: f32(inp[nm]) for nm, _ in _WNAMES}
    shared["final_norm"] = f32(inp["final_norm"])
    shared["invf"] = invf
    sw = np.asarray(inp["hy_short_w"], dtype=np.float32).reshape(DEPTH, 3, 3, 2, 128)
    shared["hsw"] = np.ascontiguousarray(sw.transpose(0, 3, 2, 1, 4))
    sb = np.asarray(inp["hy_short_b"], dtype=np.float32).reshape(DEPTH, 3, 2, 128)
    shared["hsb"] = np.ascontiguousarray(sb.transpose(0, 2, 1, 3))
    w4 = np.asarray(inp["hy_filt_w4"], dtype=np.float32).reshape(DEPTH, 64, 4, 2, 128)
    shared["hw4"] = np.ascontiguousarray(w4.transpose(0, 3, 1, 2, 4))
    bd = np.asarray(inp["hy_bias_d"], dtype=np.float32).reshape(DEPTH, 2, 2, 128)
    shared["hbd"] = np.ascontiguousarray(bd.transpose(0, 2, 1, 3))
    shared["c_dft"] = cst["c_dft"]; shared["c_tw"] = cst["c_tw"]; shared["c_zT"] = cst["c_zT"]; shared["c_dec"] = cst["c_dec"]
    if "F" not in _PROG_CACHE:
        _PROG_CACHE["F"] = build_fused()
    nc = _PROG_CACHE["F"]
    maps = []
    for c in range(NCORES):
        b = c % BATCH
        m = dict(shared)
        r = c // BATCH
        m["x"] = np.ascontiguousarray(x[b])
        m["pos"] = np.ascontiguousarray(pos[b].reshape(SEQ, 1))
        m["posq"] = np.ascontiguousarray(pos[b, r * (SEQ // 2):(r + 1) * (SEQ // 2)].reshape(SEQ // 2, 1))
        m["mk"] = np.array([1.0 - r, float(r)], np.float32)
        maps.append(m)
    res = run_bass_kernel_spmd(nc, maps, core_ids=list(range(NCORES))).results
    out = np.zeros((BATCH, SEQ, 1024), np.float32)
    for c in range(NCORES):
        b, r = c % BATCH, c // BATCH
        out[b, r * (SEQ // 2):(r + 1) * (SEQ // 2)] = res[c]["out"]
    return out
```

```python
import contextlib
import numpy as np
import concourse.bass as bass
import concourse.mybir as mybir

F32 = mybir.dt.float32
BF16 = mybir.dt.bfloat16
I32 = mybir.dt.int32
AF = mybir.ActivationFunctionType
ALU = mybir.AluOpType
AX = mybir.AxisListType


class Buf:
    __slots__ = ("name", "w", "r")

    def __init__(self, name):
        self.name = name
        self.w = None
        self.r = []


class View:
    __slots__ = ("ap", "bufs")

    def __init__(self, ap, bufs):
        self.ap = ap
        self.bufs = bufs

    def __getitem__(self, idx):
        return View(self.ap[idx], self.bufs)

    def re(self, pattern, **kw):
        return View(self.ap.rearrange(pattern, **kw), self.bufs)

    def bc(self, shape):
        return View(self.ap.broadcast_to(shape), self.bufs)


class Eng:
    def __init__(self, name, h, sem):
        self.name = name
        self.h = h
        self.sem = sem
        self.cnt = 0
        self.seen = {}
        self.pending = False


class Prog:
    def __init__(self, nc, n_dma_sems=24):
        self.nc = nc
        self.es = contextlib.ExitStack()
        self.E = {}
        for nm, h in (("pe", nc.tensor), ("act", nc.scalar), ("dve", nc.vector),
                      ("pool", nc.gpsimd), ("sp", nc.sync)):
            sem = self.es.enter_context(nc.semaphore("s_" + nm))
            self.E[nm] = Eng(nm, h, sem)
        self.dsems = {}
        self.dnext = {}
        for q, n in (("sp", 14), ("pool", 14), ("act", 4)):
            self.dsems[q] = []
            self.dnext[q] = 0
            for i in range(n):
                sem = self.es.enter_context(nc.semaphore("d%s%d" % (q, i)))
                self.dsems[q].append([sem, 0])
        self.spare = [self.es.enter_context(nc.semaphore("sp%d" % i)) for i in range(40)]
        self.uid = 0
        self.out_events = []

    def _nm(self, name):
        self.uid += 1
        return "%s_%d" % (name, self.uid)

    def sb(self, name, shape, dtype, es=None):
        name = self._nm(name)
        t = (es or self.es).enter_context(self.nc.sbuf_tensor(name, list(shape), dtype))
        return View(t.ap() if hasattr(t, "ap") else t[:], [Buf(name)])

    def sbn(self, name, shape, dtype, n, es=None):
        shp = [shape[0], n] + list(shape[1:])
        name = self._nm(name)
        t = (es or self.es).enter_context(self.nc.sbuf_tensor(name, shp, dtype))
        ap = t.ap() if hasattr(t, "ap") else t[:]
        return [View(ap[:, i], [Buf("%s%d" % (name, i))]) for i in range(n)]

    def ps(self, name, shape, dtype, es=None):
        name = self._nm(name)
        t = (es or self.es).enter_context(self.nc.psum_tensor(name, list(shape), dtype))
        ap = t.ap() if hasattr(t, "ap") else t[:]
        return View(ap, [Buf(name)])

    def _wait(self, e, ev):
        if ev is None:
            return
        key, sem, val = ev
        if e.seen.get(key, 0) >= val:
            return
        e.h.wait_ge(sem, val)
        e.seen[key] = val

    def _deps(self, e, reads, writes, pe_acc=False):
        for v in reads:
            for b in v.bufs:
                self._wait(e, b.w)
        for v in writes:
            for b in v.bufs:
                if not (pe_acc and b.w is not None and b.w[0].startswith("pe:")):
                    self._wait(e, b.w)
                for ev in b.r:
                    self._wait(e, ev)

    def _record(self, ev, reads, writes):
        for v in reads:
            for b in v.bufs:
                b.r.append(ev)
                if len(b.r) > 12:
                    best = {}
                    for x in b.r:
                        if x[0] not in best or best[x[0]][2] < x[2]:
                            best[x[0]] = x
                    b.r = list(best.values())
        for v in writes:
            for b in v.bufs:
                b.w = ev
                b.r = []

    def op(self, eng, fn, *, reads=(), writes=(), signal=True, pe_acc=False, **kw):
        e = self.E[eng]
        rd = list(reads)
        wr = list(writes)
        args = {}
        for k, v in kw.items():
            if isinstance(v, View):
                if k in ("out", "accum_out"):
                    wr.append(v)
                else:
                    rd.append(v)
                args[k] = v.ap
            else:
                args[k] = v
        self._deps(e, rd, wr, pe_acc=pe_acc)
        ins = getattr(e.h, fn)(**args)
        if signal:
            ins.then_inc(e.sem, 1)
            e.cnt += 1
            ev = ("%s:%d" % (eng, e.sem.num), e.sem, e.cnt)
        else:
            ev = ("%s:%d" % (eng, e.sem.num), e.sem, e.cnt + 1)
        self._record(ev, rd, wr)
        return ev

    def mm(self, out, lhsT, rhs, start, stop, **kw):
        return self.op("pe", "matmul", out=out, lhsT=lhsT, rhs=rhs, start=start, stop=stop,
                       signal=bool(stop), pe_acc=True, **kw)

    def transpose(self, out, in_, identity, signal=True):
        return self.op("pe", "transpose", out=out, in_=in_, identity=identity, signal=signal, pe_acc=True)

    def dma(self, q, out, in_, dram_out=False, **kw):
        e = self.E[q]
        slot = self.dsems[q][self.dnext[q]]
        self.dnext[q] = (self.dnext[q] + 1) % len(self.dsems[q])
        sem, val = slot
        key = "d%d" % sem.num
        if val > 0:
            self._wait(e, (key, sem, val))
        rd, wr = [], []
        o = out.ap if isinstance(out, View) else out
        i = in_.ap if isinstance(in_, View) else in_
        if isinstance(out, View):
            wr.append(out)
        if isinstance(in_, View):
            rd.append(in_)
        self._deps(e, rd, wr)
        e.h.dma_start(out=o, in_=i, **kw).then_inc(sem, 16)
        slot[1] = val + 16
        ev = (key, sem, val + 16)
        self._record(ev, rd, wr)
        if dram_out:
            self.out_events.append(ev)
        return ev

    def wait_event(self, eng, ev):
        self._wait(self.E[eng], ev)

    def finish(self, eng="sp"):
        e = self.E[eng]
        for ev in self.out_events:
            self._wait(e, ev)
        self.out_events = []

    def barrier(self):
        evs = []
        for nm, e in self.E.items():
            if e.cnt > 0:
                evs.append(("%s:%d" % (nm, e.sem.num), e.sem, e.cnt))
        for q in self.dsems:
            for sem, val in self.dsems[q]:
                if val > 0:
                    evs.append(("d%d" % sem.num, sem, val))
        for nm, e in self.E.items():
            for ev in evs:
                self._wait(e, ev)
        for nm, e in self.E.items():
            if e.cnt > 12000 and self.spare:
                e.sem = self.spare.pop()
                e.cnt = 0

    def collective(self, kind, in_ap, out_ap, groups):
        self.barrier()
        e = self.E["pool"]
        slot = self.dsems["pool"][self.dnext["pool"]]
        self.dnext["pool"] = (self.dnext["pool"] + 1) % len(self.dsems["pool"])
        sem, val = slot
        key = "d%d" % sem.num
        if val > 0:
            self._wait(e, (key, sem, val))
        ins = e.h.collective_compute(kind, op=ALU.bypass, replica_groups=groups, ins=[in_ap], outs=[out_ap])
        ins.then_inc(sem, 16)
        slot[1] = val + 16
        self.barrier()

    def close(self):
        self.es.close()
D = 1024
DFF = 2816
NFC = 22
EPS = 1e-6


def make_ident(P, es, dtype=BF16, name="ident"):
    ident = P.sb(name, [128, 128], dtype, es=es)
    P.op("pool", "memset", ap=ident.ap, constant=0.0, writes=[ident])
    P.op("pool", "affine_select", out=ident, in_=ident, pattern=[[-1, 128]],
         compare_op=ALU.not_equal, fill=1.0, base=0, channel_multiplier=1)
    return ident


def load_mask(P, es, mk_ap):
    mk = P.sb("mk", [128, 2], F32, es=es)
    P.dma("sp", mk, mk_ap.partition_broadcast(128))
    return mk


def blend(P, dst, alt, mk):
    np_ = dst.ap.shape[0]
    P.op("dve", "tensor_scalar", out=dst, in0=dst, scalar1=mk[0:np_, 0:1], scalar2=None, op0=ALU.mult)
    P.op("dve", "scalar_tensor_tensor", out=dst, in0=alt, scalar=mk[0:np_, 1:2], in1=dst, op0=ALU.mult, op1=ALU.add)


def rms_scale(P, ss, rs):
    P.op("dve", "tensor_scalar", out=rs, in0=ss, scalar1=EPS, scalar2=None, op0=ALU.add)
    P.op("act", "activation", out=rs, in_=rs, func=AF.Sqrt)
    P.op("dve", "reciprocal", out=rs, in_=rs)


def norm_transpose(P, xt, gB, xn, junk, ss, rs, pt, xnT_dst, ident, nd=8, groups=None):
    n = nd * 128
    groups = groups or [(0, n)]
    ng = len(groups)
    for gi, (a, b) in enumerate(groups):
        P.op("act", "activation", out=xn[:, a:b], in_=xt[:, a:b], func=AF.Square, scale=float((b - a) ** -0.5),
             accum_out=ss[:, gi:gi + 1])
    rms_scale(P, ss[:, :ng], rs[:, :ng])
    for gi, (a, b) in enumerate(groups):
        P.op("dve", "scalar_tensor_tensor", out=xn[:, a:b], in0=xt[:, a:b], scalar=rs[:, gi:gi + 1], in1=gB[:, a:b],
             op0=ALU.mult, op1=ALU.mult)
    for dc in range(nd):
        P.transpose(pt[:, dc, :], xn[:, dc * 128:(dc + 1) * 128], ident, signal=(dc == nd - 1))
    P.op("act", "copy", out=xnT_dst, in_=pt[:, :nd, :])


def stage_ffn(P, x_in, x_out, g_norm, wg, wu, wd, ntok, g_final=None):
    es = contextlib.ExitStack()
    NB = ntok // 512
    Wg = P.sbn("Wg", [128, DFF], BF16, 8, es=es)
    Wu = P.sbn("Wu", [128, DFF], BF16, 8, es=es)
    Wd = P.sbn("Wd", [128, D], BF16, NFC, es=es)
    gB = P.sb("gB", [128, D], F32, es=es)
    P.dma("sp", gB, g_norm.partition_broadcast(128))
    if g_final is not None:
        gF = P.sb("gF", [128, D], F32, es=es)
        P.dma("sp", gF, g_final.partition_broadcast(128))
    for dc in range(8):
        P.dma("pool", Wg[dc], wg[dc * 128:(dc + 1) * 128, :])
        P.dma("pool", Wu[dc], wu[dc * 128:(dc + 1) * 128, :])
    for fc in range(NFC):
        P.dma("pool", Wd[fc], wd[fc * 128:(fc + 1) * 128, :])
    ident = make_ident(P, es)
    NX = 3
    xts = P.sbn("xts", [128, D], F32, NX, es=es)
    xn = P.sbn("xn", [128, D], BF16, 2, es=es)
    junk = None
    ss = P.sbn("ss", [128, 4], F32, 2, es=es)
    rs = P.sbn("rs", [128, 4], F32, 2, es=es)
    xnT = P.sbn("xnT", [128, 8, 512], BF16, 2, es=es)
    gT = P.sbn("gT", [128, NFC, 512], BF16, 1, es=es)
    sg = P.sbn("sg", [128, 512], F32, 2, es=es)
    xo = P.sbn("xo", [128, D], F32, 2, es=es)
    pt = P.ps("pt", [128, 8, 128], BF16, es=es)
    pg = [P.ps("pg%d" % i, [128, 512], F32, es=es) for i in range(2)]
    pu = [P.ps("pu%d" % i, [128, 512], F32, es=es) for i in range(2)]
    po = [P.ps("po%d" % i, [128, 512], F32, es=es) for i in range(2)]
    ntile = ntok // 128

    def load_tile(t):
        if t < ntile:
            P.dma("sp", xts[t % NX], x_in[t * 128:(t + 1) * 128, :])

    for t in range(NX - 1):
        load_tile(t)
    cnt = 0
    for b in range(NB):
        s = b % 2
        for tt in range(4):
            t = b * 4 + tt
            load_tile(t + NX - 1)
            k = t % 2
            norm_transpose(P, xts[t % NX], gB, xn[k], junk, ss[k], rs[k], pt,
                           xnT[s][:, :, tt * 128:(tt + 1) * 128], ident)
        for fc in range(NFC):
            k = fc % 2
            for dc in range(8):
                P.mm(pg[k], Wg[dc][:, fc * 128:(fc + 1) * 128], xnT[s][:, dc, :], dc == 0, dc == 7)
            for dc in range(8):
                P.mm(pu[k], Wu[dc][:, fc * 128:(fc + 1) * 128], xnT[s][:, dc, :], dc == 0, dc == 7)
            P.op("act", "activation", out=sg[k], in_=pg[k], func=AF.Silu)
            P.op("dve", "tensor_tensor", out=gT[0][:, fc, :], in0=sg[k], in1=pu[k], op=ALU.mult)
        for tt in range(4):
            t = b * 4 + tt
            k2 = t % 2
            P.dma("sp", xo[k2], x_in[t * 128:(t + 1) * 128, :])
            for hf in range(2):
                k = cnt % 2
                cnt += 1
                for fc in range(NFC):
                    P.mm(po[k], gT[0][:, fc, tt * 128:(tt + 1) * 128], Wd[fc][:, hf * 512:(hf + 1) * 512],
                         fc == 0, fc == NFC - 1)
                P.op("dve", "scalar_tensor_tensor", out=xo[k2][:, hf * 512:(hf + 1) * 512], in0=po[k], scalar=0.5,
                     in1=xo[k2][:, hf * 512:(hf + 1) * 512], op0=ALU.mult, op1=ALU.add)
            if g_final is not None:
                P.op("act", "activation", out=xn[k2], in_=xo[k2], func=AF.Square, scale=1.0 / 32, accum_out=ss[k2][:, 0:1])
                rms_scale(P, ss[k2][:, 0:1], rs[k2][:, 0:1])
                P.op("dve", "scalar_tensor_tensor", out=xo[k2], in0=xo[k2], scalar=rs[k2][:, 0:1], in1=gF,
                     op0=ALU.mult, op1=ALU.mult)
            P.dma("pool", x_out[t * 128:(t + 1) * 128, :], xo[k2], dram_out=True)
    P.barrier()
    es.close()


NTM = 416
NFM = 1280


def stage_win(P, x_in, g_norm, w_in, h_tm, h_fm, ntok):
    es = contextlib.ExitStack()
    NB = ntok // 512
    W = P.sbn("Win", [128, 1696], BF16, 8, es=es)
    gB = P.sb("gB2", [128, D], F32, es=es)
    P.dma("sp", gB, g_norm.partition_broadcast(128))
    for dc in range(8):
        P.dma("pool", W[dc], w_in[dc * 128:(dc + 1) * 128, :])
    ident = make_ident(P, es)
    xts = P.sbn("xts", [128, D], F32, 8, es=es)
    xn = P.sbn("xn", [128, D], BF16, 2, es=es)
    junk = None
    ss = P.sbn("ss", [128, 4], F32, 2, es=es)
    rs = P.sbn("rs", [128, 4], F32, 2, es=es)
    xnT = P.sbn("xnT", [128, 8, 512], BF16, 2, es=es)
    ho = P.sbn("ho", [128, 512], F32, 4, es=es)
    pt = P.ps("pt", [128, 8, 128], BF16, es=es)
    pp = [P.ps("pp%d" % i, [128, 512], F32, es=es) for i in range(4)]

    def load_block(b):
        for tt in range(4):
            P.dma("sp", xts[(b % 2) * 4 + tt], x_in[b * 512 + tt * 128: b * 512 + (tt + 1) * 128, :])

    load_block(0)
    cnt = 0
    for b in range(NB):
        s = b % 2
        if b + 1 < NB:
            load_block(b + 1)
        for tt in range(4):
            k = (b * 4 + tt) % 2
            norm_transpose(P, xts[s * 4 + tt], gB, xn[k], junk, ss[k], rs[k], pt,
                           xnT[s][:, :, tt * 128:(tt + 1) * 128], ident)
        for tt in range(4):
            k = cnt % 4
            cnt += 1
            for dc in range(8):
                P.mm(pp[k][:, :NTM], xnT[s][:, dc, tt * 128:(tt + 1) * 128], W[dc][:, 0:NTM], dc == 0, dc == 7)
            P.op("act", "copy", out=ho[k][:, :NTM], in_=pp[k][:, :NTM])
            P.dma("pool", h_tm[b * 512 + tt * 128: b * 512 + (tt + 1) * 128, :], ho[k][:, :NTM], dram_out=True)
        for cc in range(NFM // 128):
            k = cnt % 4
            cnt += 1
            for dc in range(8):
                P.mm(pp[k], W[dc][:, NTM + cc * 128: NTM + (cc + 1) * 128], xnT[s][:, dc, :], dc == 0, dc == 7)
            if cc % 2 == 0:
                P.op("act", "copy", out=ho[k], in_=pp[k])
            else:
                P.op("dve", "tensor_copy", out=ho[k], in_=pp[k])
            P.dma("pool", h_fm[cc * 128:(cc + 1) * 128, b * 512:(b + 1) * 512], ho[k], dram_out=True)
    P.barrier()
    es.close()


def stage_wout(P, x_in, x_out, y_parts, g_out, w_out, ntok, y_cm=None, sel=None):
    es = contextlib.ExitStack()
    W = P.sbn("Wout", [128, D], BF16, 8, es=es)
    gB = P.sb("gB3", [128, D], F32, es=es)
    P.dma("sp", gB, g_out.partition_broadcast(128))
    for dc in range(8):
        P.dma("pool", W[dc], w_out[dc * 128:(dc + 1) * 128, :])
    ident = make_ident(P, es)
    yt = P.sbn("yt", [128, D], F32, 3, es=es)
    xn = P.sbn("xn", [128, D], BF16, 2, es=es)
    ss = P.sbn("ss", [128, 4], F32, 2, es=es)
    rs = P.sbn("rs", [128, 4], F32, 2, es=es)
    ynT = P.sbn("ynT", [128, 8, 128], BF16, 2, es=es)
    xo = P.sbn("xo", [128, D], F32, 2, es=es)
    pt = P.ps("pt", [128, 8, 128], BF16, es=es)
    po = [P.ps("po%d" % i, [128, 512], F32, es=es) for i in range(4)]
    ntile = ntok // 128
    groups = [(o, o + w) for (_, o, w) in y_parts]
    if y_cm is not None:
        identf = make_ident(P, es, F32, "identf")
        ncm = y_cm[2] // 128
        ycm = P.sbn("ycm", [128, ncm, 128], F32, 3, es=es)
        pcm = P.ps("pcm", [128, ncm, 128], F32, es=es)
        groups.append((y_cm[1], y_cm[1] + y_cm[2]))

    if sel is not None:
        mk = load_mask(P, es, sel[0])
        off = sel[1]
        xo2 = P.sbn("xo2", [128, D], F32, 2, es=es)
        if y_cm is not None:
            ycm2 = P.sbn("ycm2", [128, ncm, 128], F32, 2, es=es)

    def load_tile(t):
        if t < ntile:
            for (ap, o, w) in y_parts:
                P.dma("sp", yt[t % 3][:, o:o + w], ap[t * 128:(t + 1) * 128, :])
            if y_cm is not None:
                P.dma("sp", ycm[t % 3], y_cm[0][:, t * 128:(t + 1) * 128].rearrange("(c p) t -> p c t", p=128))
                if sel is not None:
                    P.dma("sp", ycm2[t % 2], y_cm[0][:, off + t * 128:off + (t + 1) * 128].rearrange("(c p) t -> p c t", p=128))
                    blend(P, ycm[t % 3], ycm2[t % 2], mk)

    load_tile(0)
    load_tile(1)
    cnt = 0
    for t in range(ntile):
        load_tile(t + 2)
        k = t % 2
        P.dma("sp", xo[k], x_in[t * 128:(t + 1) * 128, :])
        if sel is not None:
            P.dma("sp", xo2[k], x_in[off + t * 128:off + (t + 1) * 128, :])
            blend(P, xo[k], xo2[k], mk)
        if y_cm is not None:
            for c in range(ncm):
                P.transpose(pcm[:, c, :], ycm[t % 3][:, c, :], identf, signal=(c == ncm - 1))
            P.op("act", "copy", out=yt[t % 3][:, y_cm[1]:y_cm[1] + y_cm[2]], in_=pcm.re("p c k -> p (c k)"))
        norm_transpose(P, yt[t % 3], gB, xn[k], None, ss[k], rs[k], pt, ynT[k], ident, groups=groups)
        for hf in range(2):
            kk = cnt % 4
            cnt += 1
            for dc in range(8):
                P.mm(po[kk], ynT[k][:, dc, :], W[dc][:, hf * 512:(hf + 1) * 512], dc == 0, dc == 7)
            P.op("dve", "tensor_tensor", out=xo[k][:, hf * 512:(hf + 1) * 512], in0=po[kk],
                 in1=xo[k][:, hf * 512:(hf + 1) * 512], op=ALU.add)
        P.dma("pool", x_out[t * 128:(t + 1) * 128, :], xo[k], dram_out=True)
    P.barrier()
    es.close()
import math
NH = 8
STOP = 0
S1PART = 0
QK = 96
PI = 3.14159265358979


def sin_rr(P, out, x, ti, tf, off=0.0):
    C1 = 6.28125
    C2 = 2 * math.pi - C1
    P.op("dve", "tensor_scalar", out=tf, in0=x, scalar1=1.0 / (2 * math.pi), scalar2=off / (2 * math.pi), op0=ALU.mult, op1=ALU.add)
    P.op("dve", "tensor_copy", out=ti, in_=tf)
    P.op("dve", "tensor_copy", out=tf, in_=ti)
    P.op("dve", "scalar_tensor_tensor", out=out, in0=tf, scalar=-C1, in1=x, op0=ALU.mult, op1=ALU.add)
    P.op("dve", "scalar_tensor_tensor", out=out, in0=tf, scalar=-C2, in1=out, op0=ALU.mult, op1=ALU.add)
    P.op("dve", "tensor_scalar", out=out, in0=out, scalar1=off, scalar2=-3.14159, op0=ALU.add, op1=ALU.max)
    P.op("dve", "tensor_scalar", out=out, in0=out, scalar1=3.14159, scalar2=None, op0=ALU.min)
    P.op("act", "activation", out=out, in_=out, func=AF.Sin)


def rope_tables(P, pos_i, posf, invfB, ang, cs, sn, ti, tf):
    P.op("dve", "tensor_copy", out=posf, in_=pos_i)
    P.op("dve", "tensor_scalar", out=ang, in0=invfB, scalar1=posf, scalar2=None, op0=ALU.mult)
    sin_rr(P, sn, ang, ti, tf, 0.0)
    sin_rr(P, cs, ang, ti, tf, math.pi / 2)


def rope_tables_all(P, es, pos_ap, NT_, invfB, cs_all, sn_all, tag):
    pi_ = P.sb("rpi" + tag, [128, NT_], I32, es=es)
    P.dma("sp", pi_, pos_ap.rearrange("(t p) o -> p (t o)", p=128), allow_slow_non_contiguous=True)
    pf_ = P.sb("rpf" + tag, [128, NT_], F32, es=es)
    P.op("dve", "tensor_copy", out=pf_, in_=pi_)
    CH = 16
    ang = P.sb("rang" + tag, [128, CH, 16], F32, es=es)
    ti = P.sb("rti" + tag, [128, CH * 16], I32, es=es)
    tf = P.sb("rtf" + tag, [128, CH * 16], F32, es=es)
    for t0 in range(0, NT_, CH):
        n = min(CH, NT_ - t0)
        P.op("dve", "tensor_tensor", out=ang[:, 0:n, :],
             in0=pf_[:, t0:t0 + n].re("p (t o) -> p t o", o=1).bc([128, n, 16]),
             in1=invfB.re("p (o d) -> p o d", o=1).bc([128, n, 16]), op=ALU.mult)
        af = ang[:, 0:n, :].re("p t d -> p (t d)")
        sin_rr(P, sn_all[:, t0:t0 + n, :].re("p t d -> p (t d)"), af, ti[:, 0:n * 16], tf[:, 0:n * 16], 0.0)
        sin_rr(P, cs_all[:, t0:t0 + n, :].re("p t d -> p (t d)"), af, ti[:, 0:n * 16], tf[:, 0:n * 16], math.pi / 2)


def stage_mla(P, h_q, h_kv, pos_k, pos_q, invf, g_q, w_qb, g_kv, w_kvb, y_out, NQ, NK, sel=None, HG=8):
    es = contextlib.ExitStack()
    NKT = NK // 128
    NQT = NQ // 128
    NQB = NQ // 512
    SC = float(QK ** -0.5)
    identb = make_ident(P, es, BF16, "identb")
    identf = make_ident(P, es, F32, "identf")
    Wq = P.sbn("Wq", [128, 768], BF16, 2, es=es)
    for c in range(2):
        P.dma("pool", Wq[c], w_qb[c * 128:(c + 1) * 128, :])
    Wkv = P.sb("Wkv", [128, 1024], BF16, es=es)
    P.dma("pool", Wkv, w_kvb)
    gq = P.sb("gq", [128, 256], F32, es=es)
    P.dma("sp", gq, g_q.partition_broadcast(128))
    gkv = P.sb("gkv", [128, 128], F32, es=es)
    P.dma("sp", gkv, g_kv.partition_broadcast(128))
    invfB = P.sb("invfB", [128, 16], F32, es=es)
    P.dma("sp", invfB, invf.partition_broadcast(128))
    kvnT = P.sb("kvnT", [128, NK], BF16, es=es)
    kT = P.sbn("kT", [128, NK], BF16, 2, es=es)
    vA = P.sbn("vA", [128, NKT, 65], BF16, 2, es=es)
    qT = P.sb("qT", [128, HG, NQ], BF16, es=es)
    for s in range(2):
        P.op("pool", "memset", ap=kT[s][96:97, :].ap, constant=1.0, writes=[kT[s]])
        P.op("pool", "memset", ap=vA[s][:, :, 64:65].ap, constant=1.0, writes=[vA[s]])
    ptb = [P.ps("ptb%d" % i, [128, 8, 128], BF16, es=es) for i in range(2)]
    pss2 = [P.ps("pss%d" % i, [128, 2, 512], F32, es=es) for i in range(2)]
    pss = [pss2[0][:, 0, :], pss2[0][:, 1, :], pss2[1][:, 0, :], pss2[1][:, 1, :]]
    pso = [P.ps("pso0", [128, 512], F32, es=es)]
    pmisc = P.ps("pmisc", [128, 512], F32, es=es)

    if STOP == -1:
        P.barrier(); es.close(); return
    es1 = es
    kpeT = P.sb("kpeT", [32, NK], BF16, es=es1)
    hk = P.sbn("hk", [128, 160], F32, 3, es=es1)
    pk = P.sbn("pk", [128, 1], I32, 3, es=es1)
    kb = P.sbn("kb", [128, 256], BF16, 2, es=es1)
    for k in range(2):
        P.op("pool", "memset", ap=kb[k].ap, constant=0.0, writes=[kb[k]])
    ss = P.sbn("ss", [128, 4], F32, 2, es=es1)
    rs = P.sbn("rs", [128, 4], F32, 2, es=es1)
    posf = P.sbn("posf", [128, 1], F32, 2, es=es1)
    ang = P.sbn("ang", [128, 16], F32, 2, es=es1)
    cs = P.sbn("cs", [128, 16], F32, 2, es=es1)
    sn = P.sbn("sn", [128, 16], F32, 2, es=es1)
    ti = P.sbn("ti", [128, 16], I32, 2, es=es1)
    tf = P.sbn("tf", [128, 16], F32, 2, es=es1)
    t1 = P.sbn("t1", [128, 16], F32, 2, es=es1)
    t2 = P.sbn("t2", [128, 16], F32, 2, es=es1)

    NTM = max(NKT, NQT)
    cs_all = P.sb("cs_all", [128, NTM, 16], F32, es=es)
    sn_all = P.sb("sn_all", [128, NTM, 16], F32, es=es)
    rope_tables_all(P, es, pos_k, NKT, invfB, cs_all, sn_all, "k")

    def loadk(t):
        if t < NKT:
            P.dma("sp", hk[t % 3], h_kv[t * 128:(t + 1) * 128, :])

    loadk(0)
    loadk(1)
    for t in range(NKT):
        loadk(t + 2)
        k = t % 2
        x = hk[t % 3]
        csk = cs_all[:, t, :]
        snk = sn_all[:, t, :]
        P.op("act", "activation", out=kb[k][:, 0:128], in_=x[:, 0:128], func=AF.Square, scale=float(128 ** -0.5),
             accum_out=ss[k][:, 0:1])
        rms_scale(P, ss[k][:, 0:1], rs[k][:, 0:1])
        P.op("dve", "scalar_tensor_tensor", out=kb[k][:, 0:128], in0=x[:, 0:128], scalar=rs[k][:, 0:1], in1=gkv,
             op0=ALU.mult, op1=ALU.mult)
        if S1PART == 2:
            continue
        x1 = x[:, 128:144]
        x2 = x[:, 144:160]
        P.op("dve", "tensor_tensor", out=t1[k], in0=x1, in1=csk, op=ALU.mult)
        P.op("dve", "tensor_tensor", out=t2[k], in0=x2, in1=snk, op=ALU.mult)
        P.op("dve", "tensor_tensor", out=kb[k][:, 128:144], in0=t1[k], in1=t2[k], op=ALU.subtract)
        P.op("dve", "tensor_tensor", out=t1[k], in0=x1, in1=snk, op=ALU.mult)
        P.op("dve", "tensor_tensor", out=t2[k], in0=x2, in1=csk, op=ALU.mult)
        P.op("dve", "tensor_tensor", out=kb[k][:, 144:160], in0=t1[k], in1=t2[k], op=ALU.add)
        if S1PART == 3:
            continue
        pt = ptb[k]
        P.transpose(pt[:, 0, :], kb[k][:, 0:128], identb, signal=False)
        P.transpose(pt[:, 1, :], kb[k][:, 128:256], identb, signal=True)
        P.op("act", "copy", out=kvnT[:, t * 128:(t + 1) * 128], in_=pt[:, 0, :])
        P.op("act", "copy", out=kpeT[0:32, t * 128:(t + 1) * 128], in_=pt[0:32, 1, :])
    for s in range(2):
        P.dma("sp", kT[s][64:96, :], kpeT[0:32, :])
    if STOP == 1:
        P.barrier(); es.close(); return

    es2 = es
    NG = NH // HG
    NCOL = HG * QK
    NHF = NCOL // 384
    hq = P.sbn("hq", [128, 256], F32, 3, es=es2)
    qn = P.sbn("qn", [128, 256], BF16, 2, es=es2)
    qnT = P.sbn("qnT", [128, 2, 128], BF16, 2, es=es2)
    qf = P.sbn("qf", [128, HG, QK], F32, 2, es=es2)
    qa = P.sbn("qa", [128, HG, 128], BF16, 2, es=es2)
    ss = P.sbn("ssq", [128, 4], F32, 2, es=es2)
    rs = P.sbn("rsq", [128, 4], F32, 2, es=es2)
    t1 = P.sbn("t1q", [128, HG, 16], F32, 2, es=es2)
    t2 = P.sbn("t2q", [128, HG, 16], F32, 2, es=es2)
    for k in range(2):
        P.op("pool", "memset", ap=qa[k].ap, constant=0.0, writes=[qa[k]])
    if sel is not None:
        mk = load_mask(P, es, sel[0])
        hq2 = P.sbn("hq2", [128, 256], F32, 2, es=es2)
    rope_tables_all(P, es, pos_q, NQT, invfB, cs_all, sn_all, "q")
    pT = P.sbn("pT", [128, 2, 512], BF16, 3, es=es)
    oT = P.sbn("oT", [128, 512], F32, 2, es=es)
    for k in range(2):
        P.op("pool", "memset", ap=oT[k].ap, constant=0.0, writes=[oT[k]])
    yo = P.sbn("yo", [128, 4, 64], F32, 2, es=es)
    rc = P.sbn("rc", [128, 4], F32, 2, es=es)
    ci = 0

    def loadq(t):
        if t < NQT:
            P.dma("sp", hq[t % 3], h_q[t * 128:(t + 1) * 128, :])
            if sel is not None:
                P.dma("sp", hq2[t % 2], sel[1][t * 128:(t + 1) * 128, :])
                blend(P, hq[t % 3], hq2[t % 2], mk)

    for hg in range(NG):
        loadq(0)
        loadq(1)
        for t in range(NQT):
            loadq(t + 2)
            k = t % 2
            x = hq[t % 3]
            pt = ptb[k]
            norm_transpose(P, x, gq, qn[k], None, ss[k], rs[k], pt, qnT[k], identb, nd=2)
            for hf in range(NHF):
                po = pss[(t * 2 + hf) % 4]
                c0 = hg * NCOL + hf * 384
                for c in range(2):
                    P.mm(po[:, 0:384], qnT[k][:, c, :], Wq[c][:, c0:c0 + 384], c == 0, c == 1)
                P.op("act", "copy", out=qf[k][:, hf * 4:(hf + 1) * 4, :], in_=po[:, 0:384].re("p (h d) -> p h d", h=4))
            P.op("act", "copy", out=qa[k][:, :, 0:64], in_=qf[k][:, :, 0:64])
            x1 = qf[k][:, :, 64:80]
            x2 = qf[k][:, :, 80:96]
            csb = cs_all[:, t, :].re("p (o d) -> p o d", o=1).bc([128, HG, 16])
            snb = sn_all[:, t, :].re("p (o d) -> p o d", o=1).bc([128, HG, 16])
            P.op("dve", "tensor_tensor", out=t1[k], in0=x1, in1=csb, op=ALU.mult)
            P.op("dve", "tensor_tensor", out=t2[k], in0=x2, in1=snb, op=ALU.mult)
            P.op("dve", "tensor_tensor", out=qa[k][:, :, 64:80], in0=t1[k], in1=t2[k], op=ALU.subtract)
            P.op("dve", "tensor_tensor", out=t1[k], in0=x1, in1=snb, op=ALU.mult)
            P.op("dve", "tensor_tensor", out=t2[k], in0=x2, in1=csb, op=ALU.mult)
            P.op("dve", "tensor_tensor", out=qa[k][:, :, 80:96], in0=t1[k], in1=t2[k], op=ALU.add)
            pt2 = ptb[(k + 1) % 2]
            for hl in range(HG):
                P.transpose(pt2[:, hl, :], qa[k][:, hl, :], identb, signal=(hl == HG - 1))
            P.op("act", "copy", out=qT[0:97, :, t * 128:(t + 1) * 128], in_=pt2[0:97, 0:HG, :])
        for hl in range(HG):
            h = hg * HG + hl
            s = h % 2
            for kb_ in range(NK // 512):
                pp = pss[kb_ % 4]
                P.mm(pp[0:64, :], Wkv[:, h * 128:h * 128 + 64], kvnT[:, kb_ * 512:(kb_ + 1) * 512], True, True)
                if kb_ % 2 == 0:
                    P.op("act", "copy", out=kT[s][0:64, kb_ * 512:(kb_ + 1) * 512], in_=pp[0:64, :])
                else:
                    P.op("dve", "tensor_copy", out=kT[s][0:64, kb_ * 512:(kb_ + 1) * 512], in_=pp[0:64, :])
            for g in range(NKT // 8):
                pp = pss[g % 4]
                for j in range(8):
                    kt = g * 8 + j
                    P.mm(pp[:, j * 64:(j + 1) * 64], kvnT[:, kt * 128:(kt + 1) * 128],
                         Wkv[:, h * 128 + 64:h * 128 + 128], True, True)
                src = pp.re("p (j d) -> p j d", j=8)
                if g % 2 == 0:
                    P.op("act", "copy", out=vA[s][:, g * 8:(g + 1) * 8, 0:64], in_=src)
                else:
                    P.op("dve", "tensor_copy", out=vA[s][:, g * 8:(g + 1) * 8, 0:64], in_=src)
            for qb in range(NQB):
                po = pso[0]
                rhs_q = qT[0:97, hl, qb * 512:(qb + 1) * 512]
                NP2 = NKT // 2

                def S(j):
                    for u_ in range(2):
                        kt = 2 * j + u_
                        P.mm(pss2[j % 2][:, u_, :], kT[s][0:97, kt * 128:(kt + 1) * 128], rhs_q, True, True)

                S(0)
                for j in range(NP2):
                    if j + 1 < NP2:
                        S(j + 1)
                    pb = pT[ci % 3]
                    ci += 1
                    P.op("act", "activation", out=pb, in_=pss2[j % 2], func=AF.Exp, scale=SC)
                    for u_ in range(2):
                        P.mm(po[0:65, :], vA[s][:, 2 * j + u_, :], pb[:, u_, :], (j == 0 and u_ == 0),
                             (j == NP2 - 1 and u_ == 1))
                k2 = qb % 2
                P.op("dve", "tensor_copy", out=oT[k2][0:65, :], in_=po[0:65, :])
                for j in range(4):
                    P.transpose(pmisc[:, j * 128:(j + 1) * 128], oT[k2][:, j * 128:(j + 1) * 128], identf,
                                signal=(j == 3))
                pv = pmisc.re("p (j d) -> p j d", j=4)
                P.op("dve", "reciprocal", out=rc[k2].re("p (j o) -> p j o", o=1), in_=pv[:, :, 64:65])
                P.op("dve", "tensor_tensor", out=yo[k2], in0=pv[:, :, 0:64],
                     in1=rc[k2].re("p (j o) -> p j o", o=1).bc([128, 4, 64]), op=ALU.mult)
                P.dma("pool", y_out[qb * 512:(qb + 1) * 512, h * 64:(h + 1) * 64].rearrange("(j p) d -> p j d", p=128),
                      yo[k2], dram_out=True)
    P.barrier()
    es.close()
CK = 31
HALO = 15


def stage_conv(P, hcT, dw_w, dw_b, ln_g, ln_b, y_out, NT, sel=None):
    es = contextlib.ExitStack()
    NP = NT + 2 * HALO
    identf = make_ident(P, es, F32, "identf")
    wT = P.sb("cwT", [128, 2, CK], F32, es=es)
    for c in range(2):
        P.dma("sp", wT[:, c, :], dw_w[:, c * 128:(c + 1) * 128].rearrange("j c -> c j"), allow_slow_non_contiguous=True)
    bT = P.sb("cbT", [128, 2], F32, es=es)
    P.dma("sp", bT, dw_b.rearrange("(c p) -> p c", p=128), allow_slow_non_contiguous=True)
    gB = P.sb("clg", [128, 256], F32, es=es)
    P.dma("sp", gB, ln_g.partition_broadcast(128))
    bB = P.sb("clb", [128, 256], F32, es=es)
    P.dma("sp", bB, ln_b.partition_broadcast(128))
    diag = P.sbn("cdiag", [128, CK, 128], BF16, 2, es=es)
    for c in range(2):
        for j in range(CK):
            P.op("dve", "tensor_scalar", out=diag[c][:, j, :], in0=identf, scalar1=wT[:, c, j:j + 1], scalar2=None,
                 op0=ALU.mult)
    u = P.sbn("cu", [128, NP], BF16, 2, es=es)
    a_t = P.sbn("ca", [128, 1024], F32, 2, es=es)
    g_t = P.sbn("cg", [128, 1024], F32, 2, es=es)
    if sel is not None:
        mk = load_mask(P, es, sel[0])
        a2 = P.sbn("ca2", [128, 1024], F32, 2, es=es)
        g2 = P.sbn("cg2", [128, 1024], F32, 2, es=es)
    i = 0
    for c in range(2):
        for o in range(0, NP, 1024):
            w = min(1024, NP - o)
            k = i % 2
            i += 1
            P.dma("sp", a_t[k][:, :w], hcT[c * 128:(c + 1) * 128, o:o + w])
            P.dma("sp", g_t[k][:, :w], hcT[256 + c * 128:256 + (c + 1) * 128, o:o + w])
            if sel is not None:
                P.dma("sp", a2[k][:, :w], sel[1][c * 128:(c + 1) * 128, o:o + w])
                P.dma("sp", g2[k][:, :w], sel[1][256 + c * 128:256 + (c + 1) * 128, o:o + w])
                blend(P, a_t[k][:, :w], a2[k][:, :w], mk)
                blend(P, g_t[k][:, :w], g2[k][:, :w], mk)
            P.op("act", "activation", out=g_t[k][:, :w], in_=g_t[k][:, :w], func=AF.Sigmoid)
            P.op("dve", "tensor_tensor", out=u[c][:, o:o + w], in0=a_t[k][:, :w], in1=g_t[k][:, :w], op=ALU.mult)
    pc = [P.ps("pc%d" % i, [128, 512], F32, es=es) for i in range(2)]
    ptk = [P.ps("ptk%d" % i, [128, 256], F32, es=es) for i in range(2)]
    uc = P.sbn("cuc", [128, 512], F32, 4, es=es)
    xc = P.sbn("cxc", [128, 256], F32, 2, es=es)
    jk = P.sbn("cjk", [128, 256], F32, 2, es=es)
    st = P.sbn("cst", [128, 8], F32, 2, es=es)
    yo = P.sbn("cyo", [128, 256], F32, 2, es=es)
    n = 0
    for b in range(NT // 512):
        for c in range(2):
            p = pc[c]
            for j in range(CK):
                P.mm(p, diag[c][:, j, :], u[c][:, b * 512 + j: b * 512 + j + 512], j == 0, j == CK - 1)
            P.op("act", "activation", out=uc[(b % 2) * 2 + c], in_=p, func=AF.Identity, bias=bT[:, c:c + 1], scale=1.0)
        for tt in range(4):
            k = n % 2
            n += 1
            pt = ptk[k]
            for c in range(2):
                P.transpose(pt[:, c * 128:(c + 1) * 128], uc[(b % 2) * 2 + c][:, tt * 128:(tt + 1) * 128], identf,
                            signal=(c == 1))
            s = st[k]
            P.op("act", "activation", out=xc[k], in_=pt, func=AF.Identity, accum_out=s[:, 0:1])
            P.op("act", "activation", out=jk[k], in_=pt, func=AF.Square, accum_out=s[:, 1:2])
            P.op("dve", "tensor_scalar", out=s[:, 2:3], in0=s[:, 0:1], scalar1=1.0 / 256, scalar2=None, op0=ALU.mult)
            P.op("dve", "tensor_tensor", out=s[:, 3:4], in0=s[:, 2:3], in1=s[:, 2:3], op=ALU.mult)
            P.op("dve", "scalar_tensor_tensor", out=s[:, 4:5], in0=s[:, 1:2], scalar=1.0 / 256, in1=s[:, 3:4],
                 op0=ALU.mult, op1=ALU.subtract)
            rms_scale(P, s[:, 4:5], s[:, 5:6])
            P.op("dve", "tensor_scalar", out=xc[k], in0=xc[k], scalar1=s[:, 2:3], scalar2=s[:, 5:6],
                 op0=ALU.subtract, op1=ALU.mult)
            P.op("dve", "tensor_tensor", out=xc[k], in0=xc[k], in1=gB, op=ALU.mult)
            P.op("dve", "tensor_tensor", out=xc[k], in0=xc[k], in1=bB, op=ALU.add)
            P.op("act", "activation", out=yo[k], in_=xc[k], func=AF.Silu)
            t0 = b * 512 + tt * 128
            P.dma("pool", y_out[t0:t0 + 128, :], yo[k], dram_out=True)
    P.barrier()
    es.close()
LSEQ = 8192
NFFT = 16384


def hyena_consts():
    n = np.arange(128)
    ang = 2 * np.pi * np.outer(n, n) / 128.0
    C = np.cos(ang)
    S = np.sin(ang)
    angT = 2 * np.pi * np.outer(n, n) / float(NFFT)
    out = {
        "c_dft": np.concatenate([C, S, -S, -C], 1).astype(np.float32),
        "c_tw": np.concatenate([np.cos(angT), np.sin(angT)], 1).astype(np.float32),
    }
    L = LSEQ
    t = np.linspace(0.0, 1.0, L, dtype=np.float32)[:, None]
    bands = 16
    a = (2.0 * np.pi * np.arange(L, dtype=np.float32)[:, None] / L).astype(np.float32)
    fb = np.linspace(1e-4, bands - 1, bands, dtype=np.float32)[None, :]
    z = np.concatenate([t, np.cos(fb * a), -np.sin(fb * a)], -1).astype(np.float32)
    out["c_zT"] = np.ascontiguousarray(z.T)
    max_decay = math.log(1e-2) / 0.3
    min_decay = math.log(1e-2) / 1.5
    deltas = np.linspace(min_decay, max_decay, 256, dtype=np.float32)
    dec = np.exp(-t * np.abs(deltas)[None, :]).astype(np.float32)
    d4 = dec.reshape(64, 128, 2, 128).transpose(2, 1, 0, 3)
    out["c_dec"] = np.ascontiguousarray(d4)
    return out


def stage_hyena(P, hhT, sw, sb_, w1, b1, w2, b2, w3, b3, w4, freq, bias_d, c_dft, c_tw, c_zT, c_dec,
                yT, scr_u, scr_z, scr_k, STOPH=0):
    es = contextlib.ExitStack()
    L = LSEQ
    Cm = P.sb("Cm", [128, 128], BF16, es=es)
    Sm = P.sb("Sm", [128, 128], BF16, es=es)
    nSm = P.sb("nSm", [128, 128], BF16, es=es)
    nCm = P.sb("nCm", [128, 128], BF16, es=es)
    CSn = P.sb("CSn", [128, 256], BF16, es=es)
    CS = P.sb("CS", [128, 256], BF16, es=es)
    SC2 = P.sb("SC2", [128, 256], BF16, es=es)
    P.dma("pool", Cm, c_dft[:, 0:128])
    P.dma("pool", Sm, c_dft[:, 128:256])
    P.dma("pool", nSm, c_dft[:, 256:384])
    P.dma("pool", nCm, c_dft[:, 384:512])
    P.dma("pool", CSn[:, 0:128], c_dft[:, 0:128])
    P.dma("pool", CSn[:, 128:256], c_dft[:, 256:384])
    P.dma("pool", CS, c_dft[:, 0:256])
    nCS = P.sb("nCS", [128, 256], BF16, es=es)
    P.dma("pool", nCS[:, 0:128], c_dft[:, 384:512])
    P.dma("pool", nCS[:, 128:256], c_dft[:, 256:384])
    P.dma("pool", SC2[:, 0:128], c_dft[:, 256:384])
    P.dma("pool", SC2[:, 128:256], c_dft[:, 0:128])
    Tw = P.sb("Tw", [128, 256], F32, es=es)
    P.dma("sp", Tw, c_tw)
    Tc = Tw[:, 0:128].re("p (o k) -> p o k", o=1).bc([128, 4, 128])
    Ts = Tw[:, 128:256].re("p (o k) -> p o k", o=1).bc([128, 4, 128])
    ones = P.sb("ones", [128, 128], F32, es=es)
    P.op("pool", "memset", ap=ones.ap, constant=1.0, writes=[ones])
    pA = P.ps("pA", [128, 4, 2, 128], F32, es=es)
    pXr = P.ps("pXr", [128, 4, 128], F32, es=es)
    pXi = P.ps("pXi", [128, 4, 128], F32, es=es)
    pG = P.ps("pG", [128, 4, 2, 128], F32, es=es)
    pY = P.ps("pY", [128, 4, 128], F32, es=es)
    pM = P.ps("pM", [128, 512], F32, es=es)
    QA = [[P.sb("QA%d_%d" % (k, i), [128, 4, 128], BF16, es=es) for i in range(4)] for k in range(2)]

    def prods(Q, Ar, Ai, Wr, Wi):
        P.op("dve", "tensor_tensor", out=Q[0], in0=Ar, in1=Wr, op=ALU.mult)
        P.op("dve", "tensor_tensor", out=Q[1], in0=Ai, in1=Wi, op=ALU.mult)
        P.op("dve", "tensor_tensor", out=Q[2], in0=Ai, in1=Wr, op=ALU.mult)
        P.op("dve", "tensor_tensor", out=Q[3], in0=Ar, in1=Wi, op=ALU.mult)

    def fwd_s1(xb, pa):
        for ch in range(4):
            P.mm(pa[:, ch].re("p a k -> p (a k)"), xb[:, ch, :], CSn[0:64, :], True, True)

    def fwd_s2(k, pa):
        prods(QA[k], pa[:, :, 0, :], pa[:, :, 1, :], Tc, Ts)

    def fwd_stage3(k, first, last, conj_part=False, px=None):
        q = [QA[k][i].re("p c k -> p (c k)") for i in range(4)]
        pxr, pxi = px if px is not None else (pXr, pXi)
        xr = pxr.re("p c k -> p (c k)")
        xi = pxi.re("p c k -> p (c k)")
        P.mm(xr, Cm, q[0], first, False)
        P.mm(xr, Cm, q[1], False, False)
        P.mm(xr, Sm, q[2], False, False)
        P.mm(xr, nSm, q[3], False, last)
        if not conj_part:
            P.mm(xi, Cm, q[2], first, False)
            P.mm(xi, nCm, q[3], False, False)
            P.mm(xi, nSm, q[0], False, False)
            P.mm(xi, nSm, q[1], False, last)
        else:
            P.mm(xi, nCm, q[2], first, False)
            P.mm(xi, Cm, q[3], False, False)
            P.mm(xi, Sm, q[0], False, False)
            P.mm(xi, Sm, q[1], False, last)

    swT = P.sb("swT", [128, 3, 3], F32, es=es)
    P.dma("sp", swT, sw.rearrange("a j c -> c a j"), allow_slow_non_contiguous=True)
    sbT = P.sb("sbT", [128, 3], F32, es=es)
    P.dma("sp", sbT, sb_.rearrange("a c -> c a"), allow_slow_non_contiguous=True)
    es0 = contextlib.ExitStack()
    hin = P.sbn("hin", [128, 2050], F32, 2, es=es0)
    hou = P.sbn("hou", [128, 2048], F32, 2, es=es0)
    i = 0
    for a in range(3):
        for blk in range(L // 2048):
            k = i % 2
            i += 1
            P.dma("sp", hin[k], hhT[a][:, blk * 2048: blk * 2048 + 2050])
            P.op("dve", "tensor_scalar", out=hou[k], in0=hin[k][:, 0:2048], scalar1=swT[:, a, 0:1], scalar2=sbT[:, a:a + 1],
                 op0=ALU.mult, op1=ALU.add)
            P.op("dve", "scalar_tensor_tensor", out=hou[k], in0=hin[k][:, 1:2049], scalar=swT[:, a, 1:2], in1=hou[k],
                 op0=ALU.mult, op1=ALU.add)
            P.op("dve", "scalar_tensor_tensor", out=hou[k], in0=hin[k][:, 2:2050], scalar=swT[:, a, 2:3], in1=hou[k],
                 op0=ALU.mult, op1=ALU.add)
            P.dma("pool", scr_u[a, :, blk * 2048:(blk + 1) * 2048], hou[k])
    P.barrier()
    es0.close()
    if STOPH == 1:
        P.barrier(); es.close(); return

    sinv = P.sb("sinv", [128, 2, 128], F32, es=es)
    esf = contextlib.ExitStack()
    w1t = P.sb("w1t", [33, 64], F32, es=esf)
    P.dma("sp", w1t, w1)
    w2t = P.sb("w2t", [64, 64], F32, es=esf)
    P.dma("sp", w2t, w2)
    w3t = P.sb("w3t", [64, 64], F32, es=esf)
    P.dma("sp", w3t, w3)
    w4t = P.sb("w4t", [64, 4, 128], F32, es=esf)
    P.dma("sp", w4t, w4)
    fr = P.sb("fr", [64, 1], F32, es=esf)
    P.dma("sp", fr, freq.rearrange("(p o) -> p o", o=1))
    bb = P.sb("bb", [64, 3], F32, es=esf)
    for j, bj in enumerate((b1, b2, b3)):
        P.dma("sp", bb[:, j:j + 1], bj.rearrange("(p o) -> p o", o=1))
    fbb = P.sb("fbb", [64, 3], F32, es=esf)
    P.op("dve", "tensor_scalar", out=fbb, in0=bb, scalar1=fr, scalar2=None, op0=ALU.mult)
    zT = P.sb("zT", [33, L], F32, es=esf)
    for q in range(4):
        P.dma("sp", zT[:, q * 2048:(q + 1) * 2048], c_zT[:, q * 2048:(q + 1) * 2048])
    hA = P.sb("hA", [64, L], F32, es=esf)
    hB = P.sb("hB", [64, L], F32, es=esf)
    arg = P.sbn("farg", [64, 512], F32, 2, es=esf)
    ti = P.sbn("fti", [64, 512], I32, 2, es=esf)
    tf = P.sbn("ftf", [64, 512], F32, 2, es=esf)
    srcs = [(zT, w1t, 33), (hA, w2t, 64), (hB, w3t, 64)]
    dsts = [hA, hB, hA]
    n = 0
    for li in range(3):
        src, wt, kk = srcs[li]
        dst = dsts[li]
        for blk in range(L // 512):
            k = n % 2
            n += 1
            P.mm(pM[0:64, :], wt[0:kk, :], src[0:kk, blk * 512:(blk + 1) * 512], True, True)
            P.op("act", "activation", out=arg[k], in_=pM[0:64, :], func=AF.Identity, bias=fbb[:, li:li + 1], scale=fr)
            sin_rr(P, dst[:, blk * 512:(blk + 1) * 512], arg[k], ti[k], tf[k], 0.0)
    h3 = hA
    kf = P.sbn("kf", [64, 128, 128], BF16, 2, es=esf)
    dec = P.sbn("dec", [64, 128], F32, 3, es=esf)
    l1p = P.sb("l1p", [64, 2, 128], F32, es=esf)
    ksr = P.sbn("ksr", [128, 4, 128], F32, 2, es=esf)
    ksi = P.sbn("ksi", [128, 4, 128], F32, 2, es=esf)
    h3v = h3.re("p (n1 n2) -> p n2 n1", n2=128)
    for o in range(2):
        for n2 in range(128):
            k = n2 % 3
            P.dma("sp", dec[k], c_dec[n2])
            pr = [pM, pY.re("p c k -> p (c k)"), pXr.re("p c k -> p (c k)"), pXi.re("p c k -> p (c k)")][n2 % 4]
            P.mm(pr[0:64, 0:256], h3v[:, n2, :], w4t[:, 2 * o:2 * o + 2, :].re("p d c -> p (d c)"), True, True)
            for d in range(2):
                eng = "dve"
                P.op(eng, "tensor_tensor", out=kf[d][:, :, n2], in0=pr[0:64, d * 128:(d + 1) * 128], in1=dec[k],
                     op=ALU.mult)
        P.op("pool", "memset", ap=kf[1][0:1, :, 0:1].ap, constant=0.0, writes=[kf[1]])
        for d in range(2):
            P.op("dve", "tensor_reduce", out=l1p[:, d, :], in_=kf[d], axis=AX.X, op=ALU.add, apply_absolute_value=True)
        P.op("dve", "tensor_tensor", out=l1p[:, 0, :], in0=l1p[:, 0, :], in1=l1p[:, 1, :], op=ALU.add)
        P.mm(pM[:, 256:384], ones[0:64, :], l1p[:, 0, :], True, True)
        P.op("dve", "tensor_scalar", out=sinv[:, o, :], in0=pM[:, 256:384], scalar1=float(NFFT), scalar2=None, op0=ALU.mult)
        P.op("dve", "reciprocal", out=sinv[:, o, :], in_=sinv[:, o, :])
        pas = [pA, pG]
        pxs = [(pXr, pXi), (pY, pM.re("p (c k) -> p c k", c=4))]
        NTF = 64
        fwd_s1(kf[0][:, 0:4, :], pas[0])
        for n in range(NTF):
            g, d = n // 2, n % 2
            if n + 1 < NTF:
                g1, d1 = (n + 1) // 2, (n + 1) % 2
                fwd_s1(kf[d1][:, g1 * 4:(g1 + 1) * 4, :], pas[(n + 1) % 2])
            fwd_s2(n % 2, pas[n % 2])
            px = pxs[g % 2]
            fwd_stage3(n % 2, d == 0, d == 1, conj_part=(d == 1), px=px)
            if d == 1:
                k = g % 2
                P.op("act", "copy", out=ksr[k], in_=px[0])
                P.op("act", "copy", out=ksi[k], in_=px[1])
                P.dma("pool", scr_k[o, g, 0], ksr[k])
                P.dma("pool", scr_k[o, g, 1], ksi[k])
    P.barrier()
    esf.close()
    if STOPH == 2:
        P.barrier(); es.close(); return

    QB = [[P.sb("QB%d_%d" % (k, i), [128, 4, 128], BF16, es=es) for i in range(4)] for k in range(2)]
    QC = [[P.sb("QC%d_%d" % (k, i), [128, 4, 128], BF16, es=es) for i in range(4)] for k in range(2)]
    dB = P.sb("dB", [64, 2, 128], F32, es=es)
    P.dma("sp", dB, bias_d.rearrange("o c -> (o c)").partition_broadcast(64).rearrange("p (o c) -> p o c", o=2))
    dsB = P.sb("dsB", [64, 2, 128], F32, es=es)
    rsv = P.sb("rsv", [64, 2, 128], F32, es=es)
    P.op("dve", "reciprocal", out=rsv, in_=sinv[0:64])
    P.op("dve", "tensor_tensor", out=dsB, in0=dB, in1=rsv, op=ALU.mult)
    xin = P.sbn("xin", [64, 4, 128], F32, 2, es=es)
    xg = P.sbn("xg", [64, 4, 128], F32, 2, es=es)
    xb = P.sbn("xb", [64, 4, 128], BF16, 2, es=es)
    Kr = P.sbn("Kr", [128, 4, 128], F32, 2, es=es)
    Ki = P.sbn("Ki", [128, 4, 128], F32, 2, es=es)
    dv = P.sbn("dv", [64, 4, 128], F32, 2, es=es)
    zo = P.sbn("zo", [64, 4, 128], F32, 2, es=es)
    scr_zv = View(scr_z, [Buf("scr_z")])

    def f1(o, g):
        k = g % 2
        cs4 = slice(g * 4, (g + 1) * 4)
        if o == 0:
            P.dma("sp", xin[k], scr_u[0, cs4, :].re("c (n1 n2) -> n1 c n2", n2=128))
            P.dma("sp", xg[k], scr_u[1, cs4, :].re("c (n1 n2) -> n1 c n2", n2=128))
        else:
            P.dma("sp", xin[k], scr_zv[g])
            P.dma("sp", xg[k], scr_u[2, cs4, :].re("c (n1 n2) -> n1 c n2", n2=128))
        P.dma("sp", Kr[k], scr_k[o, g, 0])
        P.dma("sp", Ki[k], scr_k[o, g, 1])
        P.op("act", "copy", out=xb[k], in_=xin[k])
        fwd_s1(xb[k], pA)
        fwd_s2(k, pA)

    def f2(o, g):
        k = g % 2
        fwd_stage3(k, True, True)
        prods(QB[k], pXr, pXi, Kr[k], Ki[k])

    def b1(o, g):
        k = g % 2
        for ch in range(4):
            gv = pG[:, ch].re("p a k -> p (a k)")
            P.mm(gv, QB[k][0][:, ch, :], CS, True, False)
            P.mm(gv, QB[k][1][:, ch, :], nCS, False, False)
            P.mm(gv, QB[k][2][:, ch, :], SC2, False, False)
            P.mm(gv, QB[k][3][:, ch, :], SC2, False, True)
        prods(QC[k], pG[:, :, 0, :], pG[:, :, 1, :], Tc, Ts)

    def b2(o, g):
        k = g % 2
        cs4 = slice(g * 4, (g + 1) * 4)
        yv = pY[0:64].re("p c k -> p (c k)")
        qc = [QC[k][i].re("p c k -> p (c k)") for i in range(4)]
        P.mm(yv, Cm[:, 0:64], qc[0], True, False)
        P.mm(yv, nCm[:, 0:64], qc[1], False, False)
        P.mm(yv, nSm[:, 0:64], qc[2], False, False)
        P.mm(yv, nSm[:, 0:64], qc[3], False, True)
        for ch in range(4):
            c = g * 4 + ch
            P.op("act", "activation", out=dv[k][:, ch, :], in_=xin[k][:, ch, :], func=AF.Identity,
                 scale=dsB[:, o, c:c + 1])
            P.op("act", "activation", out=xg[k][:, ch, :], in_=xg[k][:, ch, :], func=AF.Identity,
                 scale=sinv[0:64, o, c:c + 1])
        P.op("dve", "tensor_tensor", out=dv[k], in0=pY[0:64], in1=dv[k], op=ALU.add)
        P.op("pool", "tensor_tensor", out=zo[k], in0=dv[k], in1=xg[k], op=ALU.mult)
        if o == 0:
            P.dma("pool", scr_zv[g], zo[k])
        else:
            P.dma("pool", yT[cs4, :].rearrange("c (n1 n2) -> n1 c n2", n2=128), zo[k], dram_out=True)

    for o in range(2):
        f1(o, 0)
        f2(o, 0)
        for g in range(32):
            if g + 1 < 32:
                f1(o, g + 1)
            b1(o, g)
            if g + 1 < 32:
                f2(o, g + 1)
            b2(o, g)
    P.barrier()
    es.close()
from concourse.bass_utils import run_bass_kernel_spmd

NCORES = 8
BATCH = 4
SEQ = 8192
DEPTH = 2
PADW = SEQ + 30
_PROG_CACHE = {}

_WNAMES = [("ffn1_norm", [1024]), ("ffn1_w_gate", [1024, 2816]), ("ffn1_w_up", [1024, 2816]), ("ffn1_w_down", [2816, 1024]),
           ("mix_norm", [1024]), ("w_in", [1024, 1696]), ("mla_q_norm", [256]), ("mla_w_qb", [256, 768]),
           ("mla_kv_norm", [128]), ("mla_w_kvb", [128, 1024]), ("conv_dw_w", [31, 256]), ("conv_dw_b", [256]),
           ("conv_ln_g", [256]), ("conv_ln_b", [256]), ("hy_filt_w1", [33, 64]), ("hy_filt_b1", [64]),
           ("hy_filt_w2", [64, 64]), ("hy_filt_b2", [64]), ("hy_filt_w3", [64, 64]), ("hy_filt_b3", [64]),
           ("hy_filt_freq", [64]), ("out_norm", [1024]), ("w_out", [1024, 1024]), ("ffn2_norm", [1024]),
           ("ffn2_w_gate", [1024, 2816]), ("ffn2_w_up", [1024, 2816]), ("ffn2_w_down", [2816, 1024])]


def _dt(nc, name, shape, kind="ExternalInput", dtype=F32):
    return nc.dram_tensor(name, list(shape), dtype, kind=kind).ap()


def build_fused():
    nc = bass.Bass("TRN2", target_bir_lowering=False)
    L = SEQ
    x = _dt(nc, "x", [L, 1024])
    pos = _dt(nc, "pos", [L, 1], dtype=I32)
    invf = _dt(nc, "invf", [16])
    Wt = {}
    for nm, shp in _WNAMES:
        Wt[nm] = _dt(nc, nm, [DEPTH] + shp)
    gf = _dt(nc, "final_norm", [1024])
    hsw = _dt(nc, "hsw", [DEPTH, 2, 3, 3, 128]); hsb = _dt(nc, "hsb", [DEPTH, 2, 3, 128])
    hw4 = _dt(nc, "hw4", [DEPTH, 2, 64, 4, 128]); hbd = _dt(nc, "hbd", [DEPTH, 2, 2, 128])
    c_dft = _dt(nc, "c_dft", [128, 512]); c_tw = _dt(nc, "c_tw", [128, 256]); c_zT = _dt(nc, "c_zT", [33, L])
    c_dec = _dt(nc, "c_dec", [2, 128, 64, 128])
    H = L // 2
    out = _dt(nc, "out", [H, 1024], "ExternalOutput")
    mk = _dt(nc, "mk", [2])
    posq = _dt(nc, "posq", [H, 1], dtype=I32)
    xa = _dt(nc, "s_xa", [L, 1024], "Internal"); xb = _dt(nc, "s_xb", [L, 1024], "Internal")
    htm = _dt(nc, "s_htm", [L, 416], "Internal"); hfm = _dt(nc, "s_hfm", [1280, PADW], "Internal")
    ymla = _dt(nc, "s_ymla", [L, 512], "Internal"); yconv = _dt(nc, "s_yconv", [L, 256], "Internal")
    yhT = _dt(nc, "s_yhT", [256, L], "Internal")
    scr_u = _dt(nc, "scr_u", [3, 128, L], "Internal"); scr_z = _dt(nc, "scr_z", [32, 64, 4, 128], "Internal")
    scr_k = _dt(nc, "scr_k", [2, 32, 2, 128, 4, 128], "Internal")
    P = Prog(nc)
    es = contextlib.ExitStack()
    zt = P.sb("zt", [128, 15], F32, es=es)
    P.op("pool", "memset", ap=zt.ap, constant=0.0, writes=[zt])
    for r in range(10):
        P.dma("sp", hfm[r * 128:(r + 1) * 128, 0:15], zt)
        P.dma("sp", hfm[r * 128:(r + 1) * 128, 15 + L:30 + L], zt)
    P.barrier()
    es.close()
    cur = x
    for i in range(DEPTH):
        last = (i == DEPTH - 1)
        w = lambda nm: Wt[nm][i]
        stage_ffn(P, cur, xa, w("ffn1_norm"), w("ffn1_w_gate"), w("ffn1_w_up"), w("ffn1_w_down"), L)
        stage_win(P, xa, w("mix_norm"), w("w_in"), htm, hfm[:, 15:15 + L], L)
        if not last:
            stage_conv(P, hfm[0:512, :], w("conv_dw_w"), w("conv_dw_b"), w("conv_ln_g"), w("conv_ln_b"), yconv, L)
            stage_mla(P, htm[:, 0:256], htm[:, 256:416], pos, pos, invf,
                      w("mla_q_norm"), w("mla_w_qb"), w("mla_kv_norm"), w("mla_w_kvb"), ymla, L, L, HG=4)
        else:
            stage_conv(P, hfm[0:512, 0:H + 30], w("conv_dw_w"), w("conv_dw_b"), w("conv_ln_g"), w("conv_ln_b"),
                       yconv[0:H, :], H, sel=(mk, hfm[0:512, H:H + H + 30]))
            stage_mla(P, htm[0:H, 0:256], htm[:, 256:416], pos, posq, invf,
                      w("mla_q_norm"), w("mla_w_qb"), w("mla_kv_norm"), w("mla_w_kvb"), ymla[0:H, :], H, L,
                      sel=(mk, htm[H:L, 0:256]))
        for hc in range(2):
            hh = [hfm[512 + a * 256 + hc * 128: 512 + a * 256 + hc * 128 + 128, 14:14 + L + 2] for a in range(3)]
            stage_hyena(P, hh, hsw[i, hc], hsb[i, hc], w("hy_filt_w1"), w("hy_filt_b1"), w("hy_filt_w2"), w("hy_filt_b2"),
                        w("hy_filt_w3"), w("hy_filt_b3"), hw4[i, hc], w("hy_filt_freq"), hbd[i, hc], c_dft, c_tw, c_zT,
                        c_dec[hc], yhT[hc * 128:(hc + 1) * 128, :], View(scr_u, [Buf("scr_u")]), scr_z,
                        View(scr_k, [Buf("scr_k")]))
        if not last:
            stage_wout(P, xa, xb, [(ymla, 0, 512), (yconv, 512, 256)], w("out_norm"), w("w_out"), L, y_cm=(yhT, 768, 256))
            stage_ffn(P, xb, xa, w("ffn2_norm"), w("ffn2_w_gate"), w("ffn2_w_up"), w("ffn2_w_down"), L)
            cur = xa
        else:
            stage_wout(P, xa, xb[0:H, :], [(ymla[0:H, :], 0, 512), (yconv[0:H, :], 512, 256)], w("out_norm"), w("w_out"), H,
                       y_cm=(yhT, 768, 256), sel=(mk, H))
            stage_ffn(P, xb[0:H, :], out, w("ffn2_norm"), w("ffn2_w_gate"), w("ffn2_w_up"), w("ffn2_w_down"), H, g_final=gf)
    P.finish(); P.close()
    return nc


def kernel(**inp):
    f32 = lambda a: np.ascontiguousarray(np.asarray(a, dtype=np.float32))
    x = f32(inp["x"])
    pos = np.ascontiguousarray(np.asarray(inp["positions"]).astype(np.int32))
    cst = hyena_consts()
    invf = (1.0 / (10000.0 ** (np.arange(0, 32, 2, dtype=np.float32) / 32))).astype(np.float32)
    shared = {nm: f32(inp[nm]) for nm, _ in _WNAMES}
    shared["final_norm"] = f32(inp["final_norm"])
    shared["invf"] = invf
    sw = np.asarray(inp["hy_short_w"], dtype=np.float32).reshape(DEPTH, 3, 3, 2, 128)
    shared["hsw"] = np.ascontiguousarray(sw.transpose(0, 3, 2, 1, 4))
    sb = np.asarray(inp["hy_short_b"], dtype=np.float32).reshape(DEPTH, 3, 2, 128)
    shared["hsb"] = np.ascontiguousarray(sb.transpose(0, 2, 1, 3))
    w4 = np.asarray(inp["hy_filt_w4"], dtype=np.float32).reshape(DEPTH, 64, 4, 2, 128)
    shared["hw4"] = np.ascontiguousarray(w4.transpose(0, 3, 1, 2, 4))
    bd = np.asarray(inp["hy_bias_d"], dtype=np.float32).reshape(DEPTH, 2, 2, 128)
    shared["hbd"] = np.ascontiguousarray(bd.transpose(0, 2, 1, 3))
    shared["c_dft"] = cst["c_dft"]; shared["c_tw"] = cst["c_tw"]; shared["c_zT"] = cst["c_zT"]; shared["c_dec"] = cst["c_dec"]
    if "F" not in _PROG_CACHE:
        _PROG_CACHE["F"] = build_fused()
    nc = _PROG_CACHE["F"]
    maps = []
    for c in range(NCORES):
        b = c % BATCH
        m = dict(shared)
        r = c // BATCH
        m["x"] = np.ascontiguousarray(x[b])
        m["pos"] = np.ascontiguousarray(pos[b].reshape(SEQ, 1))
        m["posq"] = np.ascontiguousarray(pos[b, r * (SEQ // 2):(r + 1) * (SEQ // 2)].reshape(SEQ // 2, 1))
        m["mk"] = np.array([1.0 - r, float(r)], np.float32)
        maps.append(m)
    res = run_bass_kernel_spmd(nc, maps, core_ids=list(range(NCORES))).results
    out = np.zeros((BATCH, SEQ, 1024), np.float32)
    for c in range(NCORES):
        b, r = c % BATCH, c // BATCH
        out[b, r * (SEQ // 2):(r + 1) * (SEQ // 2)] = res[c]["out"]
    return out
```
